# Optimizing a Trainium2 kernel written in Bass

```python
import math
import jax, jax.numpy as jnp
from jax import lax
import numpy as np

D_MODEL = 1024
BATCH = 16
SEQ = 2048
DEPTH = 4

CTX_LEN = 256
GRID_W = 64
FN_GROUPS = 4
FN_GROUP_DIM = 128
FN_WIDTH = FN_GROUPS * FN_GROUP_DIM
DA_HEADS = 4
DA_HEAD_DIM = 64
DA_V_DIM = 2 * DA_HEAD_DIM
DA_QK_WIDTH = DA_HEADS * 2 * DA_HEAD_DIM
DA_V_WIDTH = DA_HEADS * DA_V_DIM
DA_Q_BLOCK = 128
NA_HEADS = 8
NA_HEAD_DIM = 64
NA_WIDTH = NA_HEADS * NA_HEAD_DIM
NA_ROWS = 8
NA_COLS = 16
NA_Q_BLOCK_W = 16
NA_K_BLOCK_W = 32
ROPE_THETA = 10000.0
ROPE_AXIS_DIM = DA_HEAD_DIM // 2
N_BRANCH = 3
PROJ_SPLITS = (
    FN_WIDTH,
    FN_WIDTH + DA_QK_WIDTH,
    FN_WIDTH + 2 * DA_QK_WIDTH,
    FN_WIDTH + 2 * DA_QK_WIDTH + DA_V_WIDTH,
    FN_WIDTH + 2 * DA_QK_WIDTH + DA_V_WIDTH + NA_WIDTH,
    FN_WIDTH + 2 * DA_QK_WIDTH + DA_V_WIDTH + 2 * NA_WIDTH,
    FN_WIDTH + 2 * DA_QK_WIDTH + DA_V_WIDTH + 3 * NA_WIDTH,
)
PROJ_WIDTH = PROJ_SPLITS[-1] + N_BRANCH * D_MODEL
D_FF = 2816
CONV_W = 3
N_MOD = 6
NORM_EPS = 1e-6
SUBLN_EPS = 1e-5
NEG_INF = -1e30

kernel_name = "hybrid_fnet_diffattn_natten_dit_block"


def rms_norm(x, g, eps=NORM_EPS):
    xf = x.astype(jnp.float32)
    y = xf * lax.rsqrt(jnp.mean(xf * xf, axis=-1, keepdims=True) + eps)
    return (y * g.astype(jnp.float32)).astype(x.dtype)


def modulate(h, shift, scale):
    return h * (1.0 + scale) + shift


def axial_rope_tables(n_tokens):
    t = jnp.arange(n_tokens, dtype=jnp.int32)
    row = (t // GRID_W).astype(jnp.float32)
    col = (t % GRID_W).astype(jnp.float32)
    n_freq = ROPE_AXIS_DIM // 2
    inv = ROPE_THETA ** (-jnp.arange(n_freq, dtype=jnp.float32) / n_freq)
    ang = jnp.stack([row[:, None] * inv, col[:, None] * inv], axis=1)
    return jnp.cos(ang), jnp.sin(ang)


def apply_axial_rope(x, cos, sin):
    shp = x.shape
    xr = x.astype(jnp.float32).reshape(shp[:-1] + (2, 2, ROPE_AXIS_DIM // 2))
    a, b = xr[..., 0, :], xr[..., 1, :]
    cb = cos[None, :, None, None]
    sb = sin[None, :, None, None]
    out = jnp.stack([a * cb - b * sb, a * sb + b * cb], axis=-2)
    return out.reshape(shp).astype(x.dtype)


def dwconv3(x, w, b):
    xp = jnp.pad(x, ((0, 0), (1, 1), (0, 0)))
    return xp[:, :-2] * w[0] + xp[:, 1:-1] * w[1] + xp[:, 2:] * w[2] + b


def fourier_mix(u):
    bsz, n, _ = u.shape
    ug = u.astype(jnp.float32).reshape(bsz, n, FN_GROUPS, FN_GROUP_DIM)
    f = jnp.fft.fft2(ug, axes=(1, 3), norm="ortho").real
    return f.reshape(bsz, n, FN_WIDTH).astype(u.dtype)


def diff_attend(q, k, v, lam):
    s = jnp.einsum('bqhmd,bkhmd->bhmqk', q, k).astype(jnp.float32) * (DA_HEAD_DIM ** -0.5)
    p = jax.nn.softmax(s, axis=-1)
    w = p[:, :, 0] - lam * p[:, :, 1]
    return jnp.einsum('bhqk,bkhe->bqhe', w.astype(v.dtype), v)


def diff_attention_latent(q, k_all, v_all, lam):
    bsz, n, h, m, d = q.shape
    nb = n // DA_Q_BLOCK
    qb = q.reshape(bsz, nb, DA_Q_BLOCK, h, m, d).swapaxes(0, 1)
    out = lax.map(lambda qi: diff_attend(qi, k_all, v_all, lam), qb)
    return out.swapaxes(0, 1).reshape(bsz, n, h, DA_V_DIM)


def softmax_attend(q, k, v):
    s = jnp.einsum('bqhd,bkhd->bhqk', q, k).astype(jnp.float32) * (q.shape[-1] ** -0.5)
    p = jax.nn.softmax(s, axis=-1).astype(v.dtype)
    return jnp.einsum('bhqk,bkhe->bqhe', p, v)


def neighbourhood_attention_latent(q, k, v, k_ctx, v_ctx, rpb):
    bsz, n, h, d = q.shape
    rows = n // GRID_W
    kh = min(NA_ROWS, rows)
    ncb = GRID_W // NA_Q_BLOCK_W
    qg = (q * (d ** -0.5)).reshape(bsz, rows, ncb, NA_Q_BLOCK_W, h, d)
    kg = k.reshape(bsz, rows, GRID_W, h, d)
    vg = v.reshape(bsz, rows, GRID_W, h, d)
    qcol = np.arange(GRID_W).reshape(ncb, NA_Q_BLOCK_W)
    c0 = np.clip(qcol - NA_COLS // 2, 0, GRID_W - NA_COLS)
    kc0 = np.clip(np.arange(ncb) * NA_Q_BLOCK_W - NA_COLS // 2, 0, GRID_W - NA_K_BLOCK_W)
    kcol = kc0[:, None] + np.arange(NA_K_BLOCK_W)
    col_ok = (kcol[:, None, :] >= c0[:, :, None]) & (kcol[:, None, :] < c0[:, :, None] + NA_COLS)
    dc = np.clip(kcol[:, None, :] - qcol[:, :, None] + NA_COLS - 1, 0, 2 * NA_COLS - 2)
    bias_c = rpb.astype(jnp.float32)[:, :, dc]
    bias_c = jnp.where(jnp.asarray(col_ok)[None, None], bias_c, NEG_INF)
    bias_c = jnp.transpose(bias_c, (0, 2, 3, 1, 4))
    kcol_j = jnp.asarray(kcol, dtype=jnp.int32)
    n_lat = kh * NA_K_BLOCK_W

    def one_row(r):
        r0 = jnp.clip(r - kh // 2, 0, rows - kh)
        q_r = lax.dynamic_index_in_dim(qg, r, axis=1, keepdims=False)
        k_rows = lax.dynamic_slice_in_dim(kg, r0, kh, axis=1)
        v_rows = lax.dynamic_slice_in_dim(vg, r0, kh, axis=1)
        k_blk = k_rows[:, :, kcol_j]
        v_blk = v_rows[:, :, kcol_j]
        bias = lax.dynamic_slice_in_dim(bias_c, r0 - r + NA_ROWS - 1, kh, axis=3)
        s_lat = jnp.einsum('bjqhd,bkjwhd->bhjqkw', q_r, k_blk).astype(jnp.float32) + bias
        s_ctx = jnp.einsum('bjqhd,bchd->bhjqc', q_r, k_ctx).astype(jnp.float32)
        s = jnp.concatenate([s_lat.reshape(s_lat.shape[:4] + (n_lat,)), s_ctx], axis=-1)
        p = jax.nn.softmax(s, axis=-1).astype(v.dtype)
        p_lat = p[..., :n_lat].reshape(s_lat.shape)
        p_ctx = p[..., n_lat:]
        o = (jnp.einsum('bhjqkw,bkjwhe->bjqhe', p_lat, v_blk)
             + jnp.einsum('bhjqc,bche->bjqhe', p_ctx, v_ctx))
        return o.reshape(bsz, GRID_W, h * d)

    out = lax.map(one_row, jnp.arange(rows, dtype=jnp.int32))
    return out.swapaxes(0, 1).reshape(bsz, n, h * d)


def mixer(h_lat, h_ctx, w_in, b_gate, w_a, lam_vec, subln_g, w_b, rpb, w_c, w_out,
          lam_init, cos, sin, with_ctx_out):
    bsz = h_lat.shape[0]
    w_fa, w_bq, w_bk, w_bv, w_cq, w_ck, w_cv, w_g = jnp.split(w_in, PROJ_SPLITS, axis=1)
    lv = lam_vec.astype(jnp.float32)
    lam = jnp.exp(jnp.sum(lv[0] * lv[1])) - jnp.exp(jnp.sum(lv[2] * lv[3])) + lam_init

    def da_qk(h, w):
        return (h @ w).reshape(h.shape[0], h.shape[1], DA_HEADS, 2, DA_HEAD_DIM)

    def heads(h, w, nh, dh):
        return (h @ w).reshape(h.shape[0], h.shape[1], nh, dh)

    def diff_post(o):
        return (rms_norm(o, subln_g, SUBLN_EPS) * (1.0 - lam_init)).reshape(bsz, o.shape[1], DA_V_WIDTH)

    def merge(h, fa, db, nc):
        gates = jax.nn.sigmoid(h @ w_g + b_gate)
        g_a, g_b, g_c = jnp.split(gates, N_BRANCH, axis=-1)
        return (g_a * (fa @ w_a) + g_b * (db @ w_b) + g_c * (nc @ w_c)) @ w_out

    kb_c = da_qk(h_ctx, w_bk)
    vb_c = heads(h_ctx, w_bv, DA_HEADS, DA_V_DIM)
    kc_c = heads(h_ctx, w_ck, NA_HEADS, NA_HEAD_DIM)
    vc_c = heads(h_ctx, w_cv, NA_HEADS, NA_HEAD_DIM)

    fa = fourier_mix(h_lat @ w_fa)
    qb = apply_axial_rope(da_qk(h_lat, w_bq), cos, sin)
    kb = apply_axial_rope(da_qk(h_lat, w_bk), cos, sin)
    vb = heads(h_lat, w_bv, DA_HEADS, DA_V_DIM)
    db = diff_post(diff_attention_latent(qb, jnp.concatenate([kb, kb_c], axis=1),
                                         jnp.concatenate([vb, vb_c], axis=1), lam))
    nc = neighbourhood_attention_latent(heads(h_lat, w_cq, NA_HEADS, NA_HEAD_DIM),
                                        heads(h_lat, w_ck, NA_HEADS, NA_HEAD_DIM),
                                        heads(h_lat, w_cv, NA_HEADS, NA_HEAD_DIM),
                                        kc_c, vc_c, rpb)
    y_lat = merge(h_lat, fa, db, nc)
    if not with_ctx_out:
        return y_lat, None

    n_ctx = h_ctx.shape[1]
    fa_c = fourier_mix(h_ctx @ w_fa)
    db_c = diff_post(diff_attend(da_qk(h_ctx, w_bq), kb_c, vb_c, lam))
    nc_c = softmax_attend(heads(h_ctx, w_cq, NA_HEADS, NA_HEAD_DIM), kc_c, vc_c).reshape(bsz, n_ctx, NA_WIDTH)
    y_ctx = merge(h_ctx, fa_c, db_c, nc_c)
    return y_lat, y_ctx


def conv_ffn(h, w_up, conv_w, conv_b, w_down):
    u = dwconv3(h @ w_up, conv_w, conv_b)
    a, b = jnp.split(u, 2, axis=-1)
    return (jax.nn.silu(a) * b) @ w_down


def setup_inputs(seed: int = 0) -> dict:
    key = jax.random.key(seed)
    ks = jax.random.split(key, 24)
    f32 = jnp.float32

    def nrm(k, shape, s):
        return s * jax.random.normal(k, shape, f32)

    L, D = DEPTH, D_MODEL
    return {
        "x": nrm(ks[0], (BATCH, SEQ, D), 1.0),
        "c": nrm(ks[1], (BATCH, D), 1.0),
        "ctx": nrm(ks[2], (BATCH, CTX_LEN, D), 1.0),
        "c_ctx": nrm(ks[3], (D,), 1.0),
        "w_ada": nrm(ks[4], (L, D, N_MOD * D), 0.5 * D ** -0.5),
        "b_ada": nrm(ks[5], (L, N_MOD * D), 0.02),
        "g_mix": 1.0 + nrm(ks[6], (L, D), 0.02),
        "g_ffn": 1.0 + nrm(ks[7], (L, D), 0.02),
        "w_in": nrm(ks[8], (L, D, PROJ_WIDTH), D ** -0.5),
        "b_gate": nrm(ks[9], (L, N_BRANCH * D), 0.02),
        "w_a": nrm(ks[10], (L, FN_WIDTH, D), FN_WIDTH ** -0.5),
        "lam": nrm(ks[11], (L, 4, DA_HEAD_DIM), 0.1),
        "subln_g": 1.0 + nrm(ks[12], (L, DA_V_DIM), 0.02),
        "w_b": nrm(ks[13], (L, DA_V_WIDTH, D), DA_V_WIDTH ** -0.5),
        "rpb": nrm(ks[14], (L, NA_HEADS, 2 * NA_ROWS - 1, 2 * NA_COLS - 1), 0.1),
        "w_c": nrm(ks[15], (L, NA_WIDTH, D), NA_WIDTH ** -0.5),
        "w_out": nrm(ks[16], (L, D, D), D ** -0.5),
        "w_up": nrm(ks[17], (L, D, 2 * D_FF), D ** -0.5),
        "conv_w": nrm(ks[18], (L, CONV_W, 2 * D_FF), CONV_W ** -0.5),
        "conv_b": nrm(ks[19], (L, 2 * D_FF), 0.02),
        "w_down": nrm(ks[20], (L, D_FF, D), D_FF ** -0.5),
        "g_final": 1.0 + nrm(ks[21], (D,), 0.02),
    }


def reference(x, c, ctx, c_ctx, w_ada, b_ada, g_mix, g_ffn, w_in, b_gate, w_a, lam,
              subln_g, w_b, rpb, w_c, w_out, w_up, conv_w, conv_b, w_down, g_final):
    n_tok = x.shape[1]
    cos, sin = axial_rope_tables(n_tok)
    silu_c = jax.nn.silu(c)
    silu_cc = jax.nn.silu(c_ctx)
    h_x = x
    h_c = ctx
    for l in range(DEPTH):
        last = l == DEPTH - 1
        lam_init = 0.8 - 0.6 * math.exp(-0.3 * l)
        mod = (silu_c @ w_ada[l] + b_ada[l])[:, None, :]
        mod_c = silu_cc @ w_ada[l] + b_ada[l]
        sh1, sc1, gt1, sh2, sc2, gt2 = jnp.split(mod, N_MOD, axis=-1)
        csh1, csc1, cgt1, csh2, csc2, cgt2 = jnp.split(mod_c, N_MOD, axis=-1)

        a_x = modulate(rms_norm(h_x, g_mix[l]), sh1, sc1)
        a_c = modulate(rms_norm(h_c, g_mix[l]), csh1, csc1)
        y_x, y_c = mixer(a_x, a_c, w_in[l], b_gate[l], w_a[l], lam[l], subln_g[l], w_b[l],
                         rpb[l], w_c[l], w_out[l], lam_init, cos, sin, not last)
        h_x = h_x + gt1 * y_x
        f_x = modulate(rms_norm(h_x, g_ffn[l]), sh2, sc2)
        h_x = h_x + gt2 * conv_ffn(f_x, w_up[l], conv_w[l], conv_b[l], w_down[l])
        if not last:
            h_c = h_c + cgt1 * y_c
            f_c = modulate(rms_norm(h_c, g_ffn[l]), csh2, csc2)
            h_c = h_c + cgt2 * conv_ffn(f_c, w_up[l], conv_w[l], conv_b[l], w_down[l])
    return rms_norm(h_x, g_final)
```

```python
import contextlib
import math
import numpy as np
import ml_dtypes
import concourse.bass as bass
import concourse.mybir as mybir
from concourse.bass_utils import run_bass_kernel_spmd

F32 = mybir.dt.float32
BF16 = mybir.dt.bfloat16
AF = mybir.ActivationFunctionType
ALU = mybir.AluOpType
AX = mybir.AxisListType

N_DSEM = 8

D = 1024
NLAT = 2048
NCTX = 256
T = NLAT + NCTX
DEPTH = 4
PROJW = 6656
DFF = 2816
NFF = 22
NORM_EPS = 1e-6
SUBLN_EPS = 1e-5
TILES = [(0, 512), (512, 512), (1024, 512), (1536, 512), (2048, 256)]
GE = 30
MASKV = -30000.0


class Sched:
    ENG = ("pe", "act", "dve", "pool", "sp")

    def __init__(self, nc, stack):
        self.nc = nc
        self.q = {e: [] for e in self.ENG}
        self.sem = {e: stack.enter_context(nc.semaphore("s_" + e)) for e in self.ENG}
        self.cnt = {e: 0 for e in self.ENG}
        self.dsem = {}
        self.dcnt = {}
        self.drr = {}
        for e in ("sp", "pool"):
            self.dsem[e] = [stack.enter_context(nc.semaphore("d_%s%d" % (e, i))) for i in range(N_DSEM)]
            self.dcnt[e] = [0] * N_DSEM
            self.drr[e] = 0
        self.seen = {e: {} for e in self.ENG}
        self.state = {}
        self.n_wait = 0
        self.n_ins = 0
        self.pe_dirty = False

    def _semobj(self, k):
        return self.sem[k[1]] if k[0] == "c" else self.dsem[k[1]][k[2]]

    def _deps(self, reads, writes):
        deps = {}

        def add(ev):
            if ev is None:
                return
            k, v = ev
            if deps.get(k, -1) < v:
                deps[k] = v

        for b in reads:
            st = self.state.get(b)
            if st:
                add(st[0])
        for b in writes:
            st = self.state.get(b)
            if st:
                add(st[0])
                for k, v in st[1].items():
                    add((k, v))
        return deps

    def _record(self, ev, reads, writes):
        for b in reads:
            st = self.state.setdefault(b, [None, {}])
            k, v = ev
            if st[1].get(k, -1) < v:
                st[1][k] = v
        for b in writes:
            self.state[b] = [ev, {}]

    def _emit_waits(self, eng, deps, skip_self):
        waits = []
        seen = self.seen[eng]
        for k, v in deps.items():
            if skip_self and k == ("c", eng):
                continue
            if seen.get(k, -1) >= v:
                continue
            seen[k] = v
            waits.append((self._semobj(k), v))
        return waits

    def op(self, eng, fn, reads=(), writes=(), inc=True):
        deps = self._deps(reads, writes)
        waits = self._emit_waits(eng, deps, skip_self=(eng == "pe"))
        if eng == "pe":
            self.pe_dirty = not inc
        if inc:
            self.cnt[eng] += 1
            ev = (("c", eng), self.cnt[eng])
        else:
            ev = (("c", eng), self.cnt[eng] + 1)
        sem = self.sem[eng]
        self.n_wait += len(waits)
        self.n_ins += 1

        def emit(e, fn=fn, waits=waits, inc=inc, sem=sem):
            for s, v in waits:
                e.wait_ge(s, v)
            ins = fn(e)
            if inc:
                ins.then_inc(sem, 1)

        self.q[eng].append(emit)
        self._record(ev, reads, writes)

    def dma(self, fn, reads=(), writes=(), eng="sp"):
        deps = self._deps(reads, writes)
        i = self.drr[eng]
        self.drr[eng] = (i + 1) % N_DSEM
        k = ("d", eng, i)
        prev = self.dcnt[eng][i]
        if prev > 0 and deps.get(k, -1) < prev:
            deps[k] = prev
        waits = self._emit_waits(eng, deps, skip_self=False)
        self.dcnt[eng][i] = prev + 16
        ev = (k, prev + 16)
        sem = self.dsem[eng][i]
        self.n_wait += len(waits)
        self.n_ins += 1

        def emit(e, fn=fn, waits=waits, sem=sem):
            for s, v in waits:
                e.wait_ge(s, v)
            fn(e).then_inc(sem, 16)

        self.q[eng].append(emit)
        self._record(ev, reads, writes)

    def barrier(self):
        assert not self.pe_dirty, "barrier with un-evented PE op"
        deps = {}
        for e in self.ENG:
            if self.cnt[e] > 0:
                deps[("c", e)] = self.cnt[e]
        for e in self.dsem:
            for i in range(N_DSEM):
                if self.dcnt[e][i] > 0:
                    deps[("d", e, i)] = self.dcnt[e][i]
        for eng in self.ENG:
            waits = self._emit_waits(eng, dict(deps), skip_self=True)
            self.n_wait += len(waits)

            def emit(e, waits=waits):
                for s, v in waits:
                    e.wait_ge(s, v)

            if waits:
                self.q[eng].append(emit)
        self.state = {}

    def run(self):
        nc = self.nc
        q = self.q
        self.q = {e: [] for e in self.ENG}
        with nc.Block() as block:
            @block.sync
            def _(e):
                for f in q["sp"]:
                    f(e)

            @block.tensor
            def _(e):
                for f in q["pe"]:
                    f(e)

            @block.scalar
            def _(e):
                for f in q["act"]:
                    f(e)

            @block.vector
            def _(e):
                for f in q["dve"]:
                    f(e)

            @block.gpsimd
            def _(e):
                for f in q["pool"]:
                    f(e)


class _StopBuild(Exception):
    pass


def build_program(L=DEPTH, NB=2, dbg=False, stop=None):
    nc = bass.Bass("TRN2", target_bir_lowering=False)

    def din(name, shape, dt=F32):
        return nc.dram_tensor(name, list(shape), dt, kind="ExternalInput").ap()

    h0T = din("h0T", [NB, 8, 128, T])
    cT_d = din("cT", [128, 8, 3])
    w_ada = din("w_ada", [L, D, 6 * D])
    b_adaT = din("b_adaT", [L, 128, 48])
    g_mixT = din("g_mixT", [L, 128, 8])
    g_ffnT = din("g_ffnT", [L, 128, 8])
    g_finT = din("g_finT", [128, 8])
    w_in_f = din("w_in", [L, D, PROJW])
    b_gateT = din("b_gateT", [L, 128, 24])
    w_abc_f = [din("w_a", [L, 512, D]), din("w_b", [L, 512, D]), din("w_c", [L, 512, D])]
    w_out_f = din("w_out", [L, D, D])
    w_up_f = din("w_up", [L, D, 2 * DFF])
    w_down_f = din("w_down", [L, DFF, D])
    w_in = nc.dram_tensor("w_in_b", [L, D, PROJW], BF16).ap()
    w_abc = [nc.dram_tensor("w_%s_b" % c_, [L, 512, D], BF16).ap() for c_ in "abc"]
    w_out = nc.dram_tensor("w_out_b", [L, D, D], BF16).ap()
    w_up = nc.dram_tensor("w_up_b", [L, D, 2 * DFF], BF16).ap()
    w_down = nc.dram_tensor("w_down_b", [L, DFF, D], BF16).ap()
    lamv = din("lamv", [L, 1, 256])
    sublnT = din("sublnT", [L, 128, 1])
    conv_wT = din("conv_wT", [L, 128, 3, 2 * NFF])
    conv_bT = din("conv_bT", [L, 128, 2 * NFF])
    rpbG = din("rpbG", [L, 8, 128, GE * 64])
    gmask = din("gmask", [128, GE * 64], BF16)
    ropeC_d = din("ropeC", [128, NLAT], BF16)
    ropeS_d = din("ropeS", [128, NLAT], BF16)
    Rm_d = din("Rm", [128, 128], BF16)
    CS_d = din("CSm", [128, 256], BF16)
    d256_d = din("dft256", [128, 2, 512], BF16)
    dftN = din("dftN", [16, 128, 2, NLAT], BF16)
    kaug_d = din("kaugc", [32, T], BF16)
    qaug_d = din("qaugc", [32, T], BF16)

    outT = nc.dram_tensor("outT", [NB, 8, 128, NLAT], F32, kind="ExternalOutput").ap()
    hT = nc.dram_tensor("hT_scr", [NB, 8, 128, T], F32).ap()
    Gd = nc.dram_tensor("G_scr", [L, 8, 128, GE * 64], BF16).ap()
    dbg_out = {}
    if dbg:
        for nm, shp, dt in (("d_aT", [128, 8, T], BF16), ("d_fa", [128, 4, T], BF16), ("d_db", [128, 4, T], BF16),
                            ("d_nc", [128, 4, T], BF16), ("d_mg", [128, 8, T], BF16), ("d_mod", [128, 48, 3], F32),
                            ("d_fT", [128, 8, T], BF16), ("d_h1", [8, 128, T], F32)):
            dbg_out[nm] = nc.dram_tensor(nm, shp, dt, kind="ExternalOutput").ap()

    with contextlib.ExitStack() as st:
        S = Sched(nc, st)

        uniq = [0]

        def sbt(stack, name, shape, dt):
            uniq[0] += 1
            return stack.enter_context(nc.sbuf_tensor("%s_%d" % (name, uniq[0]), list(shape), dt))

        def MM(out, lhsT, rhs, start, stop, r, w):
            S.op("pe", lambda e: e.matmul(out, lhsT=lhsT, rhs=rhs, start=start, stop=stop), reads=r, writes=w, inc=True)

        def ACT(out, in_, func, r, w, bias=None, scale=None):
            kw = {}
            if bias is not None:
                kw["bias"] = bias
            if scale is not None:
                kw["scale"] = scale
            S.op("act", lambda e: e.activation(out=out, in_=in_, func=func, **kw), reads=r, writes=w)

        def TT(eng, out, in0, in1, op, r, w):
            S.op(eng, lambda e: e.tensor_tensor(out=out, in0=in0, in1=in1, op=op), reads=r, writes=w)

        def TS(eng, out, in0, s1, s2, op0, op1, r, w):
            if s2 is None:
                S.op(eng, lambda e: e.tensor_scalar(out=out, in0=in0, scalar1=s1, scalar2=None, op0=op0), reads=r, writes=w)
            else:
                S.op(eng, lambda e: e.tensor_scalar(out=out, in0=in0, scalar1=s1, scalar2=s2, op0=op0, op1=op1), reads=r, writes=w)

        def STT(eng, out, in0, scalar, in1, op0, op1, r, w):
            S.op(eng, lambda e: e.scalar_tensor_tensor(out=out, in0=in0, scalar=scalar, in1=in1, op0=op0, op1=op1), reads=r, writes=w)

        def CP(eng, out, in_, r, w):
            if eng == "act":
                S.op("act", lambda e: e.copy(out=out, in_=in_), reads=r, writes=w)
            else:
                S.op(eng, lambda e: e.tensor_copy(out=out, in_=in_), reads=r, writes=w)

        def RCP(out, in_, r, w):
            S.op("dve", lambda e: e.reciprocal(out=out, in_=in_), reads=r, writes=w)

        def MSET(eng, ap, val, w):
            S.op(eng, lambda e: e.memset(ap, val), reads=(), writes=w)

        def DMA(out, in_, r, w, eng="sp"):
            S.dma(lambda e: e.dma_start(out=out, in_=in_), reads=r, writes=w, eng=eng)

        evac_rr = [0]

        def EV(out, in_, r, w):
            evac_rr[0] ^= 1
            CP("act" if evac_rr[0] else "dve", out, in_, r, w)

        aT = sbt(st, "aT", [128, 8, T], BF16)
        mg = sbt(st, "mg", [128, 8 * T], BF16)
        mgv = mg[:].rearrange("p (c t) -> p c t", c=8)
        Zv = mg[:].rearrange("p (i g x) -> p i g x", i=18, g=4)
        uv = mg[:, 0:NFF * 512].rearrange("p (i t) -> p i t", i=NFF)
        br = sbt(st, "br", [128, 4, T], BF16)
        WAR = [sbt(st, "WA", [128, 12288], BF16), sbt(st, "WB", [128, 12288], BF16)]
        ones = sbt(st, "ones", [128, 128], BF16)
        ones_f = sbt(st, "ones_f", [1, 128], F32)
        ones32 = sbt(st, "ones32", [128, 128], F32)
        epsc = sbt(st, "epsc", [128, 2], F32)
        ropeC = sbt(st, "ropeC", [128, NLAT], BF16)
        ropeS = sbt(st, "ropeS", [128, NLAT], BF16)
        Rm = sbt(st, "Rm", [128, 128], BF16)
        CSm = sbt(st, "CSm", [128, 256], BF16)
        d256 = sbt(st, "d256", [128, 2, 512], BF16)
        modT = sbt(st, "modT", [128, L, 48, 3], F32)
        colA1 = sbt(st, "colA1", [128, L, 8, 3], F32)
        colA2 = sbt(st, "colA2", [128, L, 8, 3], F32)
        gfin = sbt(st, "gfin", [128, 8], F32)
        bgate = sbt(st, "bgate", [128, L, 24], F32)
        cw = sbt(st, "cw", [128, L, 3, 2 * NFF], F32)
        cb = sbt(st, "cb", [128, L, 2 * NFF], F32)
        neglam = sbt(st, "neglam", [128, L], F32)
        sgc = sbt(st, "sgc", [128, L], F32)
        PS = [st.enter_context(nc.psum_tensor("ps%d" % i, [128, 512], F32)) for i in range(8)]

        def pk(i):
            return ("ps", i)

        AK = [("aT", t) for t in range(5)]
        MK = [("mg", t) for t in range(5)]
        BK = [("br", t) for t in range(5)]

        def wview(a, k, n):
            return WAR[a][:, 0:k * n].rearrange("p (k n) -> p k n", k=k)

        def wsub(a, off, k, n):
            return WAR[a][:, off:off + k * n].rearrange("p (k n) -> p k n", k=k)

        nphase = [0]

        def phase_end():
            S.barrier()
            S.run()
            nphase[0] += 1

        def stopped():
            return stop is not None and nphase[0] >= stop

        def prologue():
          with contextlib.ExitStack() as ph:
              cT = sbt(ph, "cT", [128, 8, 3], F32)
              scT = sbt(ph, "scT", [128, 8, 3], BF16)
              sgm = sbt(ph, "sgm", [128, 8, 3], F32)
              gmx = sbt(ph, "gmx", [128, L, 8], F32)
              gff = sbt(ph, "gff", [128, L, 8], F32)
              bada = sbt(ph, "bada", [128, L, 48], F32)
              lamr = sbt(ph, "lamr", [1, L, 4, 64], F32)
              lamp = sbt(ph, "lamp", [1, L, 2, 64], F32)
              lams = sbt(ph, "lams", [1, L, 2], F32)
              lam1 = sbt(ph, "lam1", [1, L], F32)
              subl = sbt(ph, "subl", [128, L], F32)
              gf = sbt(ph, "gf", [128, GE * 64], F32)
              gm = sbt(ph, "gm", [128, GE * 64], BF16)
              gb = [sbt(ph, "gb0", [128, GE * 64], BF16), sbt(ph, "gb1", [128, GE * 64], BF16)]

              MSET("dve", ones[:], 1.0, ["ones"])
              MSET("dve", ones_f[:], 1.0, ["ones_f"])
              MSET("dve", ones32[:], 1.0, ["ones32"])
              MSET("dve", epsc[:, 0:1], NORM_EPS, ["epsc"])
              MSET("dve", epsc[:, 1:2], SUBLN_EPS, ["epsc"])
              DMA(ropeC[:], ropeC_d, [], ["ropeC"])
              DMA(ropeS[:], ropeS_d, [], ["ropeS"])
              DMA(Rm[:], Rm_d, [], ["Rm"])
              DMA(CSm[:], CS_d, [], ["CSm"])
              DMA(d256[:], d256_d, [], ["d256"])
              DMA(cT[:], cT_d, [], ["cT"])
              DMA(gfin[:], g_finT, [], ["gfin"])
              DMA(gm[:], gmask, [], ["gm"])
              for l in range(L):
                  DMA(gmx[:, l, :], g_mixT[l], [], ["gmx"])
                  DMA(gff[:, l, :], g_ffnT[l], [], ["gff"])
                  DMA(bada[:, l, :], b_adaT[l], [], ["bada"])
                  DMA(bgate[:, l, :], b_gateT[l], [], ["bgate"])
                  DMA(cw[:, l, :, :], conv_wT[l], [], ["cw"])
                  DMA(cb[:, l, :], conv_bT[l], [], ["cb"])
                  DMA(lamr[:, l, :, :], lamv[l].rearrange("o (a d) -> o a d", a=4), [], ["lamr"])
                  DMA(subl[:, l:l + 1], sublnT[l], [], ["subl"])
              for l in range(L):
                  for (dst_, src_, rows_) in ([(w_in, w_in_f, D), (w_out, w_out_f, D), (w_up, w_up_f, D), (w_down, w_down_f, DFF)]
                                              + [(w_abc[j_], w_abc_f[j_], 512) for j_ in range(3)]):
                      for r0_ in range(0, rows_, 256):
                          DMA(dst_[l, r0_:r0_ + 256, :], src_[l, r0_:r0_ + 256, :], [], [("wbf", l)], eng="pool")
              ACT(sgm[:], cT[:], AF.Sigmoid, ["cT"], ["sgm"])
              TT("dve", scT[:], cT[:], sgm[:], ALU.mult, ["cT", "sgm"], ["scT"])
              for l in range(L):
                  TT("dve", lamp[:, l, :, :], lamr[:, l, 0:4:2, :], lamr[:, l, 1:4:2, :], ALU.mult, ["lamr"], ["lamp"])
                  S.op("dve", lambda e, l=l: e.reduce_sum(out=lams[:, l, :], in_=lamp[:, l, :, :], axis=AX.X), reads=["lamp"], writes=["lams"])
              ACT(lams[:], lams[:], AF.Exp, ["lams"], ["lams"])
              for l in range(L):
                  lam_init = 0.8 - 0.6 * math.exp(-0.3 * l)
                  TT("dve", lam1[:, l:l + 1], lams[:, l, 1:2], lams[:, l, 0:1], ALU.subtract, ["lams"], ["lam1"])
                  TS("dve", lam1[:, l:l + 1], lam1[:, l:l + 1], -lam_init, None, ALU.add, None, ["lam1"], ["lam1"])
                  TS("dve", sgc[:, l:l + 1], subl[:, l:l + 1], 1.0 - lam_init, None, ALU.mult, None, ["subl"], ["sgc"])
              MM(PS[7][:, 0:L], ones_f[:], lam1[:], True, True, ["ones_f", "lam1"], [pk(7)])
              CP("dve", neglam[:], PS[7][:, 0:L], [pk(7)], ["neglam"])
              for l in range(L):
                  for nb in range(8):
                      a = nb % 2
                      wv = wview(a, 8, 768)
                      DMA(wv, w_ada[l, :, nb * 768:(nb + 1) * 768].rearrange("(k p) n -> p k n", p=128), [], [("W", a)], eng="pool")
                      for nn in range(6):
                          j = nb * 6 + nn
                          for k in range(8):
                              MM(PS[l % 2][:, j * 3:(j + 1) * 3], wv[:, k, nn * 128:(nn + 1) * 128], scT[:, k, :], k == 0, k == 7,
                                 [("W", a), "scT"], [pk(l % 2)])
                  for r in range(3):
                      TT("dve", modT[:, l, :, r], PS[l % 2][:, r:144:3], bada[:, l, :], ALU.add, [pk(l % 2), "bada"], ["modT"])
                  for r in range(3):
                      TS("dve", colA1[:, l, :, r], modT[:, l, 8:16, r], 1.0, None, ALU.add, None, ["modT"], ["colA1"])
                      TT("dve", colA1[:, l, :, r], colA1[:, l, :, r], gmx[:, l, :], ALU.mult, ["colA1", "gmx"], ["colA1"])
                      TS("dve", colA2[:, l, :, r], modT[:, l, 32:40, r], 1.0, None, ALU.add, None, ["modT"], ["colA2"])
                      TT("dve", colA2[:, l, :, r], colA2[:, l, :, r], gff[:, l, :], ALU.mult, ["colA2", "gff"], ["colA2"])
              if dbg:
                  DMA(dbg_out["d_mod"], modT[:, 0, :, :], ["modT"], ["d_mod"])
              for l in range(L):
                  for h in range(8):
                      s_ = (l * 8 + h) % 2
                      DMA(gf[:], rpbG[l, h], [], ["gf"])
                      ACT(gf[:], gf[:], AF.Exp, ["gf"], ["gf"])
                      TT("dve", gb[s_][:], gf[:], gm[:], ALU.mult, ["gf", "gm"], [("gb", s_)])
                      DMA(Gd[l, h], gb[s_][:], [("gb", s_)], [("Gd", l, h)])
              phase_end()

        def norm_tile(hs, hkey, n, Acol, Bcol, dst, dkeys, tmp, eps_i, Dn=D, nch=8):
            sq, tf, rs = tmp
            for c in range(nch):
                ACT(sq[c % 2][:, 0:n], hs[:, c, :], AF.Square, [hkey], [("n_sq", c % 2)])
                MM(PS[7][:, 0:n], ones[:], sq[c % 2][:, 0:n], c == 0, c == nch - 1, [("n_sq", c % 2), "ones"], [pk(7)])
            ACT(rs[:, 0:n], PS[7][:, 0:n], AF.Sqrt, [pk(7), "epsc"], ["n_rs"], bias=epsc[:, eps_i:eps_i + 1], scale=1.0 / Dn)
            RCP(rs[:, 0:n], rs[:, 0:n], ["n_rs"], ["n_rs"])
            for c in range(nch):
                TT("dve", tf[c % 2][:, 0:n], hs[:, c, :], rs[:, 0:n], ALU.mult, [hkey, "n_rs"], [("n_tf", c % 2)])
                if Bcol is None:
                    ACT(dst(c), tf[c % 2][:, 0:n], AF.Identity, [("n_tf", c % 2)], dkeys, scale=Acol(c))
                else:
                    ACT(dst(c), tf[c % 2][:, 0:n], AF.Identity, [("n_tf", c % 2)], dkeys, scale=Acol(c), bias=Bcol(c))

        def norm_tmp(ph):
            return ([sbt(ph, "n_sq0", [128, 512], BF16), sbt(ph, "n_sq1", [128, 512], BF16)],
                    [sbt(ph, "n_tf0", [128, 512], F32), sbt(ph, "n_tf1", [128, 512], F32)],
                    sbt(ph, "n_rs", [128, 512], F32))

        def rr_of(b, ti):
            return 2 if ti == 4 else b

        def tiles_of(l):
            return list(range(5)) if l < L - 1 else list(range(4))

        def load_cols(a, off, src, k, n, eng="pool"):
            DMA(wsub(a, off, k, n), src.rearrange("(k p) n -> p k n", p=128), [], [("W", a)], eng=eng)

        def pre_fa(l):
            load_cols(0, 0, w_in[l, :, 0:512], 8, 512)

        def pre_merge(l, j):
            load_cols(1, 0, w_in[l, :, 3584 + j * 1024:3584 + (j + 1) * 1024], 8, 1024)
            load_cols(1, 8192, w_abc[j][l], 4, 1024)

        def pre_qkv(l, base):
            for i in range(3):
                load_cols(0, i * 4096, w_in[l, :, base + i * 512:base + (i + 1) * 512], 8, 512)

        def pre_out(l):
            load_cols(0, 0, w_out[l], 8, 1024)

        def ffn_groups():
            return [(i0, min(4, NFF - i0)) for i0 in range(0, NFF, 4)]

        def pre_up(l, a, i0, gi):
            load_cols(a, 0, w_up[l, :, i0 * 128:(i0 + gi) * 128], 8, gi * 128)
            load_cols(a, 4096, w_up[l, :, DFF + i0 * 128:DFF + (i0 + gi) * 128], 8, gi * 128)

        def phase_n1_first(b, pre):
            if stopped():
                return
            with contextlib.ExitStack() as ph:
                ht = [sbt(ph, "ht0", [128, 8, 512], F32), sbt(ph, "ht1", [128, 8, 512], F32)]
                tmp = norm_tmp(ph)
                pre()
                for ti, (t0, n) in enumerate(TILES):
                    s_ = ti % 2
                    r = rr_of(b, ti)
                    DMA(ht[s_][:, :, 0:n], h0T[b, :, :, t0:t0 + n].rearrange("c p t -> p c t"), [], [("ht", s_)])
                    norm_tile(ht[s_][:, :, 0:n], ("ht", s_), n,
                              lambda c, r=r: colA1[:, 0, c, r:r + 1], lambda c, r=r: modT[:, 0, c, r:r + 1],
                              lambda c, t0=t0, n=n: aT[:, c, t0:t0 + n], [AK[ti]], tmp, 0)
                if dbg and b == 0:
                    DMA(dbg_out["d_aT"], aT[:], AK, ["d_aT"])
                phase_end()

        def phase_fa(b, l, pre):
            if stopped():
                return
            last = l == L - 1
            with contextlib.ExitStack() as ph:
                uT = sbt(ph, "uT", [128, 4, T], BF16)
                dt_ = [sbt(ph, "dft0", [128, 2, 512], BF16), sbt(ph, "dft1", [128, 2, 512], BF16), sbt(ph, "dft2", [128, 2, 512], BF16)]
                pre()
                wfa = wview(0, 8, 512)
                bank = 0
                for ti, (t0, n) in enumerate(TILES):
                    for g in range(4):
                        p = PS[bank % 4]
                        for k in range(8):
                            MM(p[:, 0:n], wfa[:, k, g * 128:(g + 1) * 128], aT[:, k, t0:t0 + n], k == 0, k == 7,
                               [("W", 0), AK[ti]], [pk(bank % 4)])
                        EV(uT[:, g, t0:t0 + n], p[:, 0:n], [pk(bank % 4)], [("uT", ti)])
                        bank += 1
                for i in range(18):
                    ti = min(i // 4, 4)
                    for half in range(2):
                        bi = 4 + (i % 2) * 2 + half
                        for gg in range(2):
                            g = half * 2 + gg
                            MM(PS[bi][:, gg * 256:(gg + 1) * 256], uT[:, g, i * 128:(i + 1) * 128], CSm[:], True, True,
                               [("uT", ti), "CSm"], [pk(bi)])
                        EV(Zv[:, i, half * 2:half * 2 + 2, :], PS[bi][:].rearrange("p (g x) -> p g x", g=2), [pk(bi)], [("Z", i)])
                sc_lat = 1.0 / math.sqrt(NLAT * 128.0)
                cnt = 0
                for j in range(4):
                    for i in range(16):
                        s_ = cnt % 3
                        cnt += 1
                        DMA(dt_[s_][:], dftN[i, :, :, j * 512:(j + 1) * 512], [], [("dft", s_)])
                        for g in range(4):
                            bi = (j % 2) * 4 + g
                            MM(PS[bi][:], Zv[:, i, g, 0:128], dt_[s_][:, 0, :], i == 0, False, [("Z", i), ("dft", s_)], [pk(bi)])
                            MM(PS[bi][:], Zv[:, i, g, 128:256], dt_[s_][:, 1, :], False, i == 15, [("Z", i), ("dft", s_)], [pk(bi)])
                    for g in range(4):
                        bi = (j % 2) * 4 + g
                        if g % 2 == 0:
                            ACT(br[:, g, j * 512:(j + 1) * 512], PS[bi][:], AF.Identity, [pk(bi)], [BK[j]], scale=sc_lat)
                        else:
                            TS("dve", br[:, g, j * 512:(j + 1) * 512], PS[bi][:], sc_lat, None, ALU.mult, None, [pk(bi)], [BK[j]])
                if not last:
                    sc_c = 1.0 / math.sqrt(NCTX * 128.0)
                    for g in range(4):
                        bi = g
                        for i in range(2):
                            MM(PS[bi][:, 0:256], Zv[:, 16 + i, g, 0:128], d256[:, i, 0:256], i == 0, False, [("Z", 16 + i), "d256"], [pk(bi)])
                            MM(PS[bi][:, 0:256], Zv[:, 16 + i, g, 128:256], d256[:, i, 256:512], False, i == 1, [("Z", 16 + i), "d256"], [pk(bi)])
                        TS("dve", br[:, g, NLAT:T], PS[bi][:, 0:256], sc_c, None, ALU.mult, None, [pk(bi)], [BK[4]])
                if dbg and b == 0 and l == 0:
                    DMA(dbg_out["d_fa"], br[:], BK, ["d_fa"])
                phase_end()

        def phase_merge(b, l, j, pre):
            if stopped():
                return
            with contextlib.ExitStack() as ph:
                gate = [sbt(ph, "gate0", [128, 512], F32), sbt(ph, "gate1", [128, 512], F32)]
                tmpm = [sbt(ph, "tmpm0", [128, 512], F32), sbt(ph, "tmpm1", [128, 512], F32)]
                pre()
                wg = wsub(1, 0, 8, 1024)
                wx = wsub(1, 8192, 4, 1024)
                cnt = 0
                for ti in tiles_of(l):
                    t0, n = TILES[ti]
                    for n8 in range(8):
                        s_ = cnt % 2
                        b0 = (cnt % 2) * 2
                        cnt += 1
                        for k in range(8):
                            MM(PS[b0][:, 0:n], wg[:, k, n8 * 128:(n8 + 1) * 128], aT[:, k, t0:t0 + n], k == 0, k == 7,
                               [("W", 1), AK[ti]], [pk(b0)])
                        for g in range(4):
                            MM(PS[b0 + 1][:, 0:n], wx[:, g, n8 * 128:(n8 + 1) * 128], br[:, g, t0:t0 + n], g == 0, g == 3,
                               [("W", 1), BK[ti]], [pk(b0 + 1)])
                        ACT(gate[s_][:, 0:n], PS[b0][:, 0:n], AF.Sigmoid, [pk(b0), "bgate"], [("gate", s_)],
                            bias=bgate[:, l, j * 8 + n8:j * 8 + n8 + 1], scale=1.0)
                        if j == 0:
                            TT("dve", mgv[:, n8, t0:t0 + n], gate[s_][:, 0:n], PS[b0 + 1][:, 0:n], ALU.mult,
                               [("gate", s_), pk(b0 + 1)], [MK[ti]])
                        else:
                            TT("dve", tmpm[s_][:, 0:n], gate[s_][:, 0:n], PS[b0 + 1][:, 0:n], ALU.mult,
                               [("gate", s_), pk(b0 + 1)], [("tmpm", s_)])
                            TT("pool", mgv[:, n8, t0:t0 + n], mgv[:, n8, t0:t0 + n], tmpm[s_][:, 0:n], ALU.add,
                               [("tmpm", s_), MK[ti]], [MK[ti]])
                if dbg and b == 0 and l == 0 and j == 2:
                    DMA(dbg_out["d_mg"], mgv, MK, ["d_mg"])
                phase_end()

        def attn_pipeline(nk, s_fn, e_fn, v_fn):
            for step in range(nk + 2):
                if step < nk:
                    s_fn(step)
                if 1 <= step < nk + 1:
                    e_fn(step - 1)
                if step >= 2:
                    v_fn(step - 2)

        def phase_da(b, l, pre):
            if stopped():
                return
            last = l == L - 1
            with contextlib.ExitStack() as ph:
                qT = [sbt(ph, "qT%d" % i, [128, T], BF16) for i in range(2)]
                kT = [sbt(ph, "kT%d" % i, [128, T], BF16) for i in range(2)]
                Vt = [sbt(ph, "Vt%d" % i, [128, 18, 128], BF16) for i in range(2)]
                xb = sbt(ph, "xb", [128, 512], BF16)
                t1 = sbt(ph, "t1", [128, 512], F32)
                t2 = sbt(ph, "t2", [128, 512], F32)
                Pb = [sbt(ph, "P%d" % i, [128, 512], BF16) for i in range(3)]
                r0 = sbt(ph, "r0", [128, 512], F32)
                r1 = sbt(ph, "r1", [128, 512], F32)
                sqo = sbt(ph, "sqo", [128, 512], BF16)
                rso = sbt(ph, "rso", [128, 512], F32)
                Pac = [[sbt(ph, "Pac%d%d" % (m_, e_), [128, 512], F32) for e_ in range(2)] for m_ in range(2)]
                pre()
                wq = wsub(0, 0, 8, 512)
                wk = wsub(0, 4096, 8, 512)
                wv = wsub(0, 8192, 8, 512)
                qtiles = tiles_of(l)

                def proj_tile(w, dstT, dkey, h, ti):
                    t0, n = TILES[ti]
                    for k in range(8):
                        MM(PS[6][:, 0:n], w[:, k, h * 128:(h + 1) * 128], aT[:, k, t0:t0 + n], k == 0, k == 7,
                           [("W", 0), AK[ti]], [pk(6)])
                    if ti < 4:
                        CP("act", xb[:, 0:n], PS[6][:, 0:n], [pk(6)], ["xb"])
                        MM(PS[7][:, 0:n], Rm[:], xb[:, 0:n], True, True, ["Rm", "xb"], [pk(7)])
                        TT("dve", t1[:, 0:n], xb[:, 0:n], ropeC[:, t0:t0 + n], ALU.mult, ["xb", "ropeC"], ["t1"])
                        TT("dve", t2[:, 0:n], PS[7][:, 0:n], ropeS[:, t0:t0 + n], ALU.mult, [pk(7), "ropeS"], ["t2"])
                        TT("pool", dstT[:, t0:t0 + n], t1[:, 0:n], t2[:, 0:n], ALU.add, ["t1", "t2"], [(dkey, ti)])
                    else:
                        CP("act", dstT[:, t0:t0 + n], PS[6][:, 0:n], [pk(6)], [(dkey, ti)])

                def v_group(h, i4, bf_):
                    ni = 4 if i4 < 4 else 2
                    for ii in range(ni):
                        i = i4 * 4 + ii
                        ti = min(i // 4, 4)
                        for k in range(8):
                            MM(PS[6][:, ii * 128:(ii + 1) * 128], aT[:, k, i * 128:(i + 1) * 128], wv[:, k, h * 128:(h + 1) * 128],
                               k == 0, k == 7, [("W", 0), AK[ti]], [pk(6)])
                    EV(Vt[bf_][:, i4 * 4:i4 * 4 + ni, :], PS[6][:, 0:ni * 128].rearrange("p (i e) -> p i e", i=ni), [pk(6)], [("Vt", bf_, i4)])

                def head_units(h):
                    bf_ = h % 2
                    u = []
                    for ti in range(5):
                        u.append(lambda ti=ti: proj_tile(wk, kT[bf_], "kT%d" % bf_, h, ti))
                    for i4 in range(5):
                        u.append(lambda i4=i4: v_group(h, i4, bf_))
                    for ti in qtiles:
                        u.append(lambda ti=ti: proj_tile(wq, qT[bf_], "qT%d" % bf_, h, ti))
                    return u

                for f_ in head_units(0):
                    f_()
                for h in range(4):
                    bf_ = h % 2
                    nxt = head_units(h + 1) if h + 1 < 4 else []
                    steps = []
                    for ti in qtiles:
                        keys = list(range(18)) if ti < 4 else [16, 17]
                        for m in range(2):
                            for sidx, i in enumerate(keys):
                                steps.append((ti, m, sidx, len(keys), i))
                    NS = len(steps)
                    every = max(1, (NS - 8) // max(1, len(nxt)))

                    def s_fn(g, bf_=bf_):
                        ti, m, sidx, nk, i = steps[g]
                        t0, n = TILES[ti]
                        kti = min(i // 4, 4)
                        MM(PS[g % 3][:, 0:n], kT[bf_][m * 64:(m + 1) * 64, i * 128:(i + 1) * 128], qT[bf_][m * 64:(m + 1) * 64, t0:t0 + n],
                           True, True, [("kT%d" % bf_, kti), ("qT%d" % bf_, ti)], [pk(g % 3)])

                    def e_fn(g):
                        ti, m, sidx, nk, i = steps[g]
                        n = TILES[ti][1]
                        ACT(Pb[g % 3][:, 0:n], PS[g % 3][:, 0:n], AF.Exp, [pk(g % 3)], [("P", g % 3)], scale=0.125)

                    def v_fn(g, bf_=bf_):
                        ti, m, sidx, nk, i = steps[g]
                        n = TILES[ti][1]
                        MM(PS[3 + m][:, 0:n], Vt[bf_][:, i, :], Pb[g % 3][:, 0:n], sidx == 0, sidx == nk - 1,
                           [("Vt", bf_, i // 4), ("P", g % 3)], [pk(3 + m)])
                        e_ = sidx % 2
                        eng_ = "pool" if e_ == 0 else "dve"
                        if sidx < 2:
                            CP(eng_, Pac[m][e_][:, 0:n], Pb[g % 3][:, 0:n], [("P", g % 3)], [("Pac", m, e_)])
                        else:
                            TT(eng_, Pac[m][e_][:, 0:n], Pac[m][e_][:, 0:n], Pb[g % 3][:, 0:n], ALU.add,
                               [("P", g % 3), ("Pac", m, e_)], [("Pac", m, e_)])
                        if sidx == nk - 1:
                            MM(PS[5][:, 0:n], ones32[:], Pac[m][0][:, 0:n], True, False, ["ones32", ("Pac", m, 0)], [pk(5)])
                            MM(PS[5][:, 0:n], ones32[:], Pac[m][1][:, 0:n], False, True, ["ones32", ("Pac", m, 1)], [pk(5)])
                            rr_ = r0 if m == 0 else r1
                            RCP(rr_[:, 0:n], PS[5][:, 0:n], [pk(5)], ["r%d" % m])

                    def fin1(ti, h=h):
                        t0, n = TILES[ti]
                        TT("dve", r0[:, 0:n], PS[3][:, 0:n], r0[:, 0:n], ALU.mult, [pk(3), "r0"], ["r0"])
                        TT("dve", r1[:, 0:n], PS[4][:, 0:n], r1[:, 0:n], ALU.mult, [pk(4), "r1"], ["r1"])
                        STT("dve", r0[:, 0:n], r1[:, 0:n], neglam[:, l:l + 1], r0[:, 0:n], ALU.mult, ALU.add, ["r0", "r1", "neglam"], ["r0"])
                        ACT(sqo[:, 0:n], r0[:, 0:n], AF.Square, ["r0"], ["sqo"])

                    def fin2(ti, h=h):
                        t0, n = TILES[ti]
                        MM(PS[7][:, 0:n], ones[:], sqo[:, 0:n], True, True, ["ones", "sqo"], [pk(7)])
                        ACT(rso[:, 0:n], PS[7][:, 0:n], AF.Sqrt, [pk(7), "epsc"], ["rso"], bias=epsc[:, 1:2], scale=1.0 / 128.0)
                        RCP(rso[:, 0:n], rso[:, 0:n], ["rso"], ["rso"])
                        TT("dve", r0[:, 0:n], r0[:, 0:n], rso[:, 0:n], ALU.mult, ["r0", "rso"], ["r0"])
                        ACT(br[:, h, t0:t0 + n], r0[:, 0:n], AF.Identity, ["r0", "sgc"], [BK[ti]], scale=sgc[:, l:l + 1])

                    later = []
                    for step in range(NS + 2):
                        if step < NS:
                            s_fn(step)
                        if 1 <= step < NS + 1:
                            e_fn(step - 1)
                        if step >= 2:
                            g = step - 2
                            ti, m, sidx, nk, i = steps[g]
                            if m == 0 and sidx == nk - 1:
                                while later:
                                    fin2(later.pop(0)[1])
                            v_fn(g)
                            if m == 1 and sidx == nk - 1:
                                fin1(ti)
                                later.append((step + 6, ti))
                        while later and later[0][0] <= step:
                            fin2(later.pop(0)[1])
                        if nxt and step >= 4 and (step - 4) % every == 0:
                            nxt.pop(0)()
                    while later:
                        fin2(later.pop(0)[1])
                    while nxt:
                        nxt.pop(0)()
                if dbg and b == 0 and l == 0:
                    DMA(dbg_out["d_db"], br[:], BK, ["d_db"])
                phase_end()

        def phase_na(b, l, pre):
            if stopped():
                return
            last = l == L - 1
            with contextlib.ExitStack() as ph:
                qa = sbt(ph, "qa", [96, T], BF16)
                ka = sbt(ph, "ka", [96, T], BF16)
                Vt = sbt(ph, "Vtn", [128, 18, 512], BF16)
                Gt = [sbt(ph, "Gt0", [128, GE * 64], BF16), sbt(ph, "Gt1", [128, GE * 64], BF16)]
                Pb = [sbt(ph, "Pn%d" % i, [128, 512], BF16) for i in range(3)]
                rn = sbt(ph, "rn", [64, 512], F32)
                Vh = [sbt(ph, "Vh%d" % i, [128, 18, 128], BF16) for i in range(2)]
                pre()
                wq = wsub(0, 0, 8, 512)
                wk = wsub(0, 4096, 8, 512)
                wv = wsub(0, 8192, 8, 512)
                qtiles = tiles_of(l)
                DMA(qa[64:96, :], qaug_d, [], ["qa_aug"])
                DMA(ka[64:96, :], kaug_d, [], ["ka_aug"])
                for i_ in range(2):
                    MSET("pool", Vh[i_][:, :, 64:128], 1.0, [("Vh", i_)])
                for i in range(18):
                    ti = min(i // 4, 4)
                    bi = 4 + i % 2
                    for k in range(8):
                        MM(PS[bi][:], aT[:, k, i * 128:(i + 1) * 128], wv[:, k, :], k == 0, k == 7, [("W", 0), AK[ti]], [pk(bi)])
                    EV(Vt[:, i, :], PS[bi][:], [pk(bi)], [("Vt", i)])
                rc = 0
                for h in range(8):
                    gs = h % 2
                    DMA(Gt[gs][:], Gd[l, h], [("Gd", l, h)], [("Gt", gs)])
                    CP("pool", Vh[gs][:, :, 0:64], Vt[:, :, h * 64:(h + 1) * 64], [("Vt", i_) for i_ in range(18)], [("Vh", gs)])
                    for (w, dst, dkey, tl) in ((wk, ka, "ka", list(range(5))), (wq, qa, "qa", qtiles)):
                        for ti in tl:
                            t0, n = TILES[ti]
                            pb_ = 6 + rc % 2
                            rc += 1
                            for k in range(8):
                                MM(PS[pb_][0:64, 0:n], w[:, k, h * 64:(h + 1) * 64], aT[:, k, t0:t0 + n], k == 0, k == 7,
                                   [("W", 0), AK[ti]], [pk(pb_)])
                            EV(dst[0:64, t0:t0 + n], PS[pb_][0:64, 0:n], [pk(pb_)], [(dkey, ti)])
                    hp = (h % 2) * 64
                    for ti in qtiles:
                        t0, n = TILES[ti]
                        if ti < 4:
                            rb = 8 * ti
                            kr_start, nlat = (0, 6) if ti == 0 else ((20, 6) if ti == 3 else (rb - 4, 8))
                            keys = [(kr_start // 2 + j_, 14 - (kr_start + 2 * j_) + rb) for j_ in range(nlat)] + [(16, None), (17, None)]
                        else:
                            keys = [(16, None), (17, None)]
                        nk = len(keys)

                        def s_fn(sidx, keys=keys, t0=t0, n=n, ti=ti):
                            i = keys[sidx][0]
                            kti = min(i // 4, 4)
                            MM(PS[sidx % 3][:, 0:n], ka[0:96, i * 128:(i + 1) * 128], qa[0:96, t0:t0 + n], True, True,
                               [("ka", kti), "ka_aug", ("qa", ti), "qa_aug"], [pk(sidx % 3)])

                        def e_fn(sidx, keys=keys, n=n, gs=gs):
                            e0 = keys[sidx][1]
                            ACT(Pb[sidx % 3][:, 0:n], PS[sidx % 3][:, 0:n], AF.Exp, [pk(sidx % 3)], [("P", sidx % 3)], scale=0.125)
                            if e0 is not None:
                                TT("dve", Pb[sidx % 3][:, 0:n], Pb[sidx % 3][:, 0:n], Gt[gs][:, e0 * 64:e0 * 64 + n], ALU.mult,
                                   [("P", sidx % 3), ("Gt", gs)], [("P", sidx % 3)])

                        def v_fn(sidx, keys=keys, n=n, nk=nk, gs=gs):
                            i = keys[sidx][0]
                            MM(PS[3][:, 0:n], Vh[gs][:, i, :], Pb[sidx % 3][:, 0:n], sidx == 0, sidx == nk - 1,
                               [("Vh", gs), ("P", sidx % 3)], [pk(3)])

                        attn_pipeline(nk, s_fn, e_fn, v_fn)
                        RCP(rn[:, 0:n], PS[3][64:128, 0:n], [pk(3)], ["rn"])
                        TT("dve", br[hp:hp + 64, h // 2, t0:t0 + n], PS[3][0:64, 0:n], rn[:, 0:n], ALU.mult, [pk(3), "rn"], [BK[ti]])
                if dbg and b == 0 and l == 0:
                    DMA(dbg_out["d_nc"], br[:], BK, ["d_nc"])
                phase_end()

        def phase_out(b, l, pre):
            if stopped():
                return
            with contextlib.ExitStack() as ph:
                ht = [sbt(ph, "ht0", [128, 8, 512], F32), sbt(ph, "ht1", [128, 8, 512], F32)]
                tmp = norm_tmp(ph)
                pre()
                wo = wsub(0, 0, 8, 1024)
                hsrc = h0T if l == 0 else hT
                cnt = 0
                for ti in tiles_of(l):
                    t0, n = TILES[ti]
                    s_ = ti % 2
                    r = rr_of(b, ti)
                    rk = [] if l == 0 else [("hT", ti)]
                    DMA(ht[s_][:, :, 0:n], hsrc[b, :, :, t0:t0 + n].rearrange("c p t -> p c t"), rk, [("ht", s_)])
                    for n8 in range(8):
                        bi = cnt % 4
                        cnt += 1
                        for k in range(8):
                            MM(PS[bi][:, 0:n], wo[:, k, n8 * 128:(n8 + 1) * 128], mgv[:, k, t0:t0 + n], k == 0, k == 7,
                               [("W", 0), MK[ti]], [pk(bi)])
                        STT("dve", ht[s_][:, n8, 0:n], PS[bi][:, 0:n], modT[:, l, 16 + n8, r:r + 1], ht[s_][:, n8, 0:n], ALU.mult, ALU.add,
                            [pk(bi), ("ht", s_), "modT"], [("ht", s_)])
                    DMA(hT[b, :, :, t0:t0 + n].rearrange("c p t -> p c t"), ht[s_][:, :, 0:n], [("ht", s_)], [("hT", ti)])
                    norm_tile(ht[s_][:, :, 0:n], ("ht", s_), n,
                              lambda c, r=r: colA2[:, l, c, r:r + 1], lambda c, r=r: modT[:, l, 24 + c, r:r + 1],
                              lambda c, t0=t0, n=n: aT[:, c, t0:t0 + n], [AK[ti]], tmp, 0)
                if dbg and b == 0 and l == 0:
                    DMA(dbg_out["d_fT"], aT[:], AK, ["d_fT"])
                phase_end()

        def phase_ffn(b, l, pre_next):
            if stopped():
                return
            last = l == L - 1
            groups = ffn_groups()
            with contextlib.ExitStack() as ph:
                ht = sbt(ph, "htf", [128, 8, 512], F32)
                tmp = norm_tmp(ph)
                hb = sbt(ph, "hb", [128, 8, 8], BF16)
                hpre = sbt(ph, "hpre", [128, 2 * NFF, 6], F32)
                pa_ = [sbt(ph, "pa0", [128, 514], F32), sbt(ph, "pa1", [128, 514], F32)]
                pb_ = [sbt(ph, "pb0", [128, 514], F32), sbt(ph, "pb1", [128, 514], F32)]
                ya = [sbt(ph, "ya0", [128, 512], F32), sbt(ph, "ya1", [128, 512], F32)]
                yb = [sbt(ph, "yb0", [128, 512], F32), sbt(ph, "yb1", [128, 512], F32)]
                wdn = [sbt(ph, "wdn%d" % i, [128, 4, 512], BF16) for i in range(2)]
                tl = tiles_of(l)
                for c_, col in enumerate((511, 512, 1023, 1024, 1535, 1536)):
                    CP("dve", hb[:, :, c_:c_ + 1], aT[:, :, col:col + 1], [AK[col // 512]], ["hb"])
                gcount = 0
                dcount = 0
                icount = 0
                for tidx, ti in enumerate(tl):
                    t0, n = TILES[ti]
                    r = rr_of(b, ti)
                    hl = (t0 // 512) * 2 - 2 if t0 not in (0, NLAT) else None
                    hr = (t0 // 512) * 2 + 1 if (t0 + n) not in (NLAT, T) else None
                    DMA(ht[:, :, 0:n], hT[b, :, :, t0:t0 + n].rearrange("c p t -> p c t"), [("hT", ti)], ["htf"])
                    pend = []
                    for gidx, (i0, gi) in enumerate(groups):
                        a = (gcount + 1) % 2
                        gcount += 1
                        if gidx + 1 < len(groups):
                            pre_up(l, (gcount + 1) % 2, *groups[gidx + 1])
                        elif tidx + 1 < len(tl):
                            pre_up(l, (gcount + 1) % 2, *groups[0])
                        wa = wsub(a, 0, 8, gi * 128)
                        wb = wsub(a, 4096, 8, gi * 128)
                        for ii in range(gi):
                            i = i0 + ii
                            s_ = icount % 2
                            icount += 1
                            for (wsel, bi, pre_t, pkey) in ((wa, 4, pa_[s_], ("pa", s_)), (wb, 5, pb_[s_], ("pb", s_))):
                                for k in range(8):
                                    MM(PS[bi][:, 0:n], wsel[:, k, ii * 128:(ii + 1) * 128], aT[:, k, t0:t0 + n], k == 0, k == 7,
                                       [("W", a), AK[ti]], [pk(bi)])
                                CP("act", pre_t[:, 1:n + 1], PS[bi][:, 0:n], [pk(bi)], [pkey])
                            if tidx == 0:
                                for (wsel, j_, co) in ((wa, i, s_ * 12), (wb, NFF + i, s_ * 12 + 6)):
                                    for k in range(8):
                                        MM(PS[6][:, co:co + 6], wsel[:, k, ii * 128:(ii + 1) * 128], hb[:, k, 0:6], k == 0, k == 7,
                                           [("W", a), "hb"], [pk(6)])
                                for (j_, co) in ((i, s_ * 12), (NFF + i, s_ * 12 + 6)):
                                    CP("dve", hpre[:, j_, :], PS[6][:, co:co + 6], [pk(6)], [("hpre", j_)])
                            for (pre_t, pkey, j_) in ((pa_[s_], ("pa", s_), i), (pb_[s_], ("pb", s_), NFF + i)):
                                for (hx, dcol) in ((hl, 0), (hr, n + 1)):
                                    if hx is None:
                                        MSET("pool", pre_t[:, dcol:dcol + 1], 0.0, [pkey])
                                    else:
                                        CP("pool", pre_t[:, dcol:dcol + 1], hpre[:, j_, hx:hx + 1], [("hpre", j_)], [pkey])
                            def chain(s_=s_, i=i, n=n):
                                for (pre_t, pkey, y_, ykey, j_) in ((pa_[s_], ("pa", s_), ya[s_], ("ya", s_), i), (pb_[s_], ("pb", s_), yb[s_], ("yb", s_), NFF + i)):
                                    ACT(y_[:, 0:n], pre_t[:, 1:n + 1], AF.Identity, [pkey, "cw", "cb"], [ykey], scale=cw[:, l, 1, j_:j_ + 1], bias=cb[:, l, j_:j_ + 1])
                                    STT("dve", y_[:, 0:n], pre_t[:, 0:n], cw[:, l, 0, j_:j_ + 1], y_[:, 0:n], ALU.mult, ALU.add, [pkey, ykey, "cw"], [ykey])
                                    STT("dve", y_[:, 0:n], pre_t[:, 2:n + 2], cw[:, l, 2, j_:j_ + 1], y_[:, 0:n], ALU.mult, ALU.add, [pkey, ykey, "cw"], [ykey])
                                ACT(ya[s_][:, 0:n], ya[s_][:, 0:n], AF.Silu, [("ya", s_)], [("ya", s_)])
                                TT("pool", uv[:, i, 0:n], ya[s_][:, 0:n], yb[s_][:, 0:n], ALU.mult, [("ya", s_), ("yb", s_)], [("u", i)])

                            if pend:
                                pend.pop()()
                            pend.append(chain)
                    if pend:
                        pend.pop()()
                    if tidx + 1 == len(tl):
                        pre_next()
                    for half in range(2):
                        for gidx, (i0, gi) in enumerate(groups):
                            ds_ = dcount % 2
                            dcount += 1
                            DMA(wdn[ds_][:, 0:gi, :], w_down[l, i0 * 128:(i0 + gi) * 128, half * 512:(half + 1) * 512].rearrange("(i p) n -> p i n", p=128),
                                [], [("wdn", ds_)], eng="pool")
                            for ii in range(gi):
                                i = i0 + ii
                                for nn in range(4):
                                    MM(PS[nn][:, 0:n], wdn[ds_][:, ii, nn * 128:(nn + 1) * 128], uv[:, i, 0:n], i == 0, i == NFF - 1,
                                       [("wdn", ds_), ("u", i)], [pk(nn)])
                        for nn in range(4):
                            n8 = half * 4 + nn
                            STT("dve", ht[:, n8, 0:n], PS[nn][:, 0:n], modT[:, l, 40 + n8, r:r + 1], ht[:, n8, 0:n], ALU.mult, ALU.add,
                                [pk(nn), "htf", "modT"], ["htf"])
                    if not last:
                        DMA(hT[b, :, :, t0:t0 + n].rearrange("c p t -> p c t"), ht[:, :, 0:n], ["htf"], [("hT", ti)])
                        norm_tile(ht[:, :, 0:n], "htf", n,
                                  lambda c, r=r: colA1[:, l + 1, c, r:r + 1], lambda c, r=r: modT[:, l + 1, c, r:r + 1],
                                  lambda c, t0=t0, n=n: aT[:, c, t0:t0 + n], [AK[ti]], tmp, 0)
                    else:
                        norm_tile(ht[:, :, 0:n], "htf", n, lambda c: gfin[:, c:c + 1], None,
                                  lambda c, n=n: ht[:, c, 0:n], ["htf"], tmp, 0)
                        DMA(outT[b, :, :, t0:t0 + n].rearrange("c p t -> p c t"), ht[:, :, 0:n], ["htf"], [("out", b, ti)])
                if dbg and b == 0 and l == 0 and not last:
                    pass
                phase_end()

        def nop():
            pass

        try:
          prologue()
          for b in range(NB):
            phase_n1_first(b, lambda: pre_fa(0))
            for l in range(L):
                phase_fa(b, l, lambda l=l: pre_merge(l, 0))
                phase_merge(b, l, 0, lambda l=l: pre_qkv(l, 512))
                phase_da(b, l, lambda l=l: pre_merge(l, 1))
                phase_merge(b, l, 1, lambda l=l: pre_qkv(l, 2048))
                phase_na(b, l, lambda l=l: pre_merge(l, 2))
                phase_merge(b, l, 2, lambda l=l: pre_out(l))
                phase_out(b, l, lambda l=l: pre_up(l, 1, *ffn_groups()[0]))
                if l + 1 < L:
                    phase_ffn(b, l, lambda l=l: pre_fa(l + 1))
                else:
                    phase_ffn(b, l, nop)
        except _StopBuild:
            pass
        S.barrier()
        S.run()
    build_program.stats = (S.n_ins, S.n_wait)
    return nc


def _consts():
    bf = ml_dtypes.bfloat16
    c = {}
    t = np.arange(NLAT)
    pos = np.stack([(t // 64).astype(np.float32), (t % 64).astype(np.float32)], 0)
    inv = (10000.0 ** (-np.arange(16, dtype=np.float32) / 16)).astype(np.float32)
    p = np.arange(128)
    axis = (p % 64) // 32
    half = (p % 32) // 16
    f = p % 16
    ang = pos[axis, :] * inv[f][:, None]
    c["ropeC"] = np.cos(ang).astype(bf)
    c["ropeS"] = np.sin(ang).astype(bf)
    R = np.zeros((128, 128), np.float32)
    for q in range(128):
        if half[q] == 0:
            R[q + 16, q] = -1.0
        else:
            R[q - 16, q] = 1.0
    c["Rm"] = R.astype(bf)
    cc = np.arange(128)
    phi = 2 * np.pi * np.outer(cc, cc) / 128.0
    c["CSm"] = np.concatenate([np.cos(phi), np.sin(phi)], 1).astype(bf)
    n2 = np.arange(NCTX)
    th = 2 * np.pi * (np.outer(n2, n2) % NCTX) / NCTX
    d256 = np.concatenate([np.cos(th), -np.sin(th)], 1).reshape(2, 128, 512).transpose(1, 0, 2)
    c["dft256"] = np.ascontiguousarray(d256).astype(bf)
    nn = np.arange(NLAT)
    thn = 2 * np.pi * ((np.outer(nn, nn) % NLAT).astype(np.float64)) / NLAT
    dN = np.stack([np.cos(thn), -np.sin(thn)], 1)
    c["dftN"] = np.ascontiguousarray(dN.reshape(16, 128, 2, NLAT)).astype(bf)
    ka = np.zeros((32, T), np.float32)
    qa = np.zeros((32, T), np.float32)
    for tok in range(NLAT):
        r = tok // 64
        ka[r, tok] = 1.0
        r0 = min(max(r - 4, 0), 24)
        qa[:, tok] = MASKV
        qa[r0:r0 + 8, tok] = 0.0
    c["kaugc"] = ka.astype(bf)
    c["qaugc"] = qa.astype(bf)
    dr = np.zeros((128, GE, 64), np.int64)
    dc = np.zeros((128, GE, 64), np.int64)
    ok = np.zeros((128, GE, 64), bool)
    for e in range(GE):
        d = 21 - e
        for pp in range(128):
            drr = d + (1 if pp >= 64 else 0)
            kc = pp % 64
            for qc in range(64):
                c0 = min(max(qc - 8, 0), 48)
                v = (0 <= drr <= 14) and (c0 <= kc < c0 + 16)
                ok[pp, e, qc] = v
                if v:
                    dr[pp, e, qc] = drr
                    dc[pp, e, qc] = kc - qc + 15
    c["_dr"] = dr.reshape(128, GE * 64)
    c["_dc"] = dc.reshape(128, GE * 64)
    c["gmask"] = ok.reshape(128, GE * 64).astype(np.float32).astype(bf)
    return c


_CONST_CACHE = {}


def _prep_inputs(inp, L, NB, ncores):
    if "c" not in _CONST_CACHE:
        _CONST_CACHE["c"] = _consts()
    C = _CONST_CACHE["c"]
    f32 = np.float32
    shared = {}
    for k in ("w_ada", "w_in", "w_a", "w_b", "w_c", "w_out", "w_up", "w_down"):
        shared[k] = np.ascontiguousarray(np.asarray(inp[k], f32)[:L])

    def fm(a, nch):
        a = np.asarray(a, f32)[:L]
        return np.ascontiguousarray(a.reshape(L, nch, 128).transpose(0, 2, 1))

    shared["b_adaT"] = fm(inp["b_ada"], 48)
    shared["g_mixT"] = fm(inp["g_mix"], 8)
    shared["g_ffnT"] = fm(inp["g_ffn"], 8)
    shared["g_finT"] = np.ascontiguousarray(np.asarray(inp["g_final"], f32).reshape(8, 128).T)
    shared["b_gateT"] = fm(inp["b_gate"], 24)
    shared["lamv"] = np.ascontiguousarray(np.asarray(inp["lam"], f32)[:L].reshape(L, 1, 256))
    shared["sublnT"] = np.ascontiguousarray(np.asarray(inp["subln_g"], f32)[:L].reshape(L, 128, 1))
    cwt = np.asarray(inp["conv_w"], f32)[:L].reshape(L, 3, 2 * NFF, 128).transpose(0, 3, 1, 2)
    shared["conv_wT"] = np.ascontiguousarray(cwt)
    shared["conv_bT"] = fm(inp["conv_b"], 2 * NFF)
    rpb = np.asarray(inp["rpb"], f32)[:L]
    shared["rpbG"] = np.ascontiguousarray(rpb[:, :, C["_dr"], C["_dc"]])
    for k in ("gmask", "ropeC", "ropeS", "Rm", "CSm", "dft256", "dftN", "kaugc", "qaugc"):
        shared[k] = C[k]
    x = np.asarray(inp["x"], f32)
    ctx = np.asarray(inp["ctx"], f32)
    c = np.asarray(inp["c"], f32)
    cc = np.asarray(inp["c_ctx"], f32)
    maps = []
    for core in range(ncores):
        m = dict(shared)
        hs = []
        for j in range(NB):
            bi = core * NB + j
            hcat = np.concatenate([x[bi], ctx[bi]], 0)
            hs.append(hcat.T.reshape(8, 128, T))
        m["h0T"] = np.ascontiguousarray(np.stack(hs, 0))
        cs = np.stack([c[core * NB + j] for j in range(NB)] + [cc], 0)
        m["cT"] = np.ascontiguousarray(cs.reshape(3, 8, 128).transpose(2, 1, 0))
        maps.append(m)
    return maps


_PROG = {}


def kernel(**inputs):
    L, NB, ncores = DEPTH, 2, 8
    if "nc" not in _PROG:
        _PROG["nc"] = build_program(L, NB)
    nc = _PROG["nc"]
    maps = _prep_inputs(inputs, L, NB, ncores)
    res = run_bass_kernel_spmd(nc, maps, core_ids=list(range(ncores)))
    outs = []
    for core in range(ncores):
        o = res.results[core]["outT"]
        for j in range(NB):
            outs.append(np.asarray(o[j], np.float32).reshape(D, NLAT).T)
    return np.ascontiguousarray(np.stack(outs, 0))
```

```python
import contextlib
import math
import numpy as np
import ml_dtypes
import concourse.bass as bass
import concourse.mybir as mybir
from concourse.bass_utils import run_bass_kernel_spmd

F32 = mybir.dt.float32
BF16 = mybir.dt.bfloat16
AF = mybir.ActivationFunctionType
ALU = mybir.AluOpType
AX = mybir.AxisListType

N_DSEM = 8

D = 1024
NLAT = 2048
NCTX = 256
T = NLAT + NCTX
DEPTH = 4
PROJW = 6656
DFF = 2816
NFF = 22
NORM_EPS = 1e-6
SUBLN_EPS = 1e-5
TILES = [(0, 512), (512, 512), (1024, 512), (1536, 512), (2048, 256)]
GE = 30
MASKV = -30000.0


class Sched:
    ENG = ("pe", "act", "dve", "pool", "sp")

    def __init__(self, nc, stack):
        self.nc = nc
        self.q = {e: [] for e in self.ENG}
        self.sem = {e: stack.enter_context(nc.semaphore("s_" + e)) for e in self.ENG}
        self.cnt = {e: 0 for e in self.ENG}
        self.dsem = {}
        self.dcnt = {}
        self.drr = {}
        for e in ("sp", "pool"):
            self.dsem[e] = [stack.enter_context(nc.semaphore("d_%s%d" % (e, i))) for i in range(N_DSEM)]
            self.dcnt[e] = [0] * N_DSEM
            self.drr[e] = 0
        self.seen = {e: {} for e in self.ENG}
        self.state = {}
        self.n_wait = 0
        self.n_ins = 0
        self.pe_dirty = False

    def _semobj(self, k):
        return self.sem[k[1]] if k[0] == "c" else self.dsem[k[1]][k[2]]

    def _deps(self, reads, writes):
        deps = {}

        def add(ev):
            if ev is None:
                return
            k, v = ev
            if deps.get(k, -1) < v:
                deps[k] = v

        for b in reads:
            st = self.state.get(b)
            if st:
                add(st[0])
        for b in writes:
            st = self.state.get(b)
            if st:
                add(st[0])
                for k, v in st[1].items():
                    add((k, v))
        return deps

    def _record(self, ev, reads, writes):
        for b in reads:
            st = self.state.setdefault(b, [None, {}])
            k, v = ev
            if st[1].get(k, -1) < v:
                st[1][k] = v
        for b in writes:
            self.state[b] = [ev, {}]

    def _emit_waits(self, eng, deps, skip_self):
        waits = []
        seen = self.seen[eng]
        for k, v in deps.items():
            if skip_self and k == ("c", eng):
                continue
            if seen.get(k, -1) >= v:
                continue
            seen[k] = v
            waits.append((self._semobj(k), v))
        return waits

    def op(self, eng, fn, reads=(), writes=(), inc=True):
        deps = self._deps(reads, writes)
        waits = self._emit_waits(eng, deps, skip_self=(eng == "pe"))
        if eng == "pe":
            self.pe_dirty = not inc
        if inc:
            self.cnt[eng] += 1
            ev = (("c", eng), self.cnt[eng])
        else:
            ev = (("c", eng), self.cnt[eng] + 1)
        sem = self.sem[eng]
        self.n_wait += len(waits)
        self.n_ins += 1

        def emit(e, fn=fn, waits=waits, inc=inc, sem=sem):
            for s, v in waits:
                e.wait_ge(s, v)
            ins = fn(e)
            if inc:
                ins.then_inc(sem, 1)

        self.q[eng].append(emit)
        self._record(ev, reads, writes)

    def dma(self, fn, reads=(), writes=(), eng="sp"):
        deps = self._deps(reads, writes)
        i = self.drr[eng]
        self.drr[eng] = (i + 1) % N_DSEM
        k = ("d", eng, i)
        prev = self.dcnt[eng][i]
        if prev > 0 and deps.get(k, -1) < prev:
            deps[k] = prev
        waits = self._emit_waits(eng, deps, skip_self=False)
        self.dcnt[eng][i] = prev + 16
        ev = (k, prev + 16)
        sem = self.dsem[eng][i]
        self.n_wait += len(waits)
        self.n_ins += 1

        def emit(e, fn=fn, waits=waits, sem=sem):
            for s, v in waits:
                e.wait_ge(s, v)
            fn(e).then_inc(sem, 16)

        self.q[eng].append(emit)
        self._record(ev, reads, writes)

    def barrier(self):
        assert not self.pe_dirty, "barrier with un-evented PE op"
        deps = {}
        for e in self.ENG:
            if self.cnt[e] > 0:
                deps[("c", e)] = self.cnt[e]
        for e in self.dsem:
            for i in range(N_DSEM):
                if self.dcnt[e][i] > 0:
                    deps[("d", e, i)] = self.dcnt[e][i]
        for eng in self.ENG:
            waits = self._emit_waits(eng, dict(deps), skip_self=True)
            self.n_wait += len(waits)

            def emit(e, waits=waits):
                for s, v in waits:
                    e.wait_ge(s, v)

            if waits:
                self.q[eng].append(emit)
        self.state = {}

    def run(self):
        nc = self.nc
        q = self.q
        self.q = {e: [] for e in self.ENG}
        with nc.Block() as block:
            @block.sync
            def _(e):
                for f in q["sp"]:
                    f(e)

            @block.tensor
            def _(e):
                for f in q["pe"]:
                    f(e)

            @block.scalar
            def _(e):
                for f in q["act"]:
                    f(e)

            @block.vector
            def _(e):
                for f in q["dve"]:
                    f(e)

            @block.gpsimd
            def _(e):
                for f in q["pool"]:
                    f(e)


class _StopBuild(Exception):
    pass


def build_program(L=DEPTH, NB=2, dbg=False, stop=None):
    nc = bass.Bass("TRN2", target_bir_lowering=False)

    def din(name, shape, dt=F32):
        return nc.dram_tensor(name, list(shape), dt, kind="ExternalInput").ap()

    h0T = din("h0T", [NB, 8, 128, T])
    cT_d = din("cT", [128, 8, 3])
    w_ada = din("w_ada", [L, D, 6 * D])
    b_adaT = din("b_adaT", [L, 128, 48])
    g_mixT = din("g_mixT", [L, 128, 8])
    g_ffnT = din("g_ffnT", [L, 128, 8])
    g_finT = din("g_finT", [128, 8])
    w_in_f = din("w_in", [L, D, PROJW])
    b_gateT = din("b_gateT", [L, 128, 24])
    w_abc_f = [din("w_a", [L, 512, D]), din("w_b", [L, 512, D]), din("w_c", [L, 512, D])]
    w_out_f = din("w_out", [L, D, D])
    w_up_f = din("w_up", [L, D, 2 * DFF])
    w_down_f = din("w_down", [L, DFF, D])
    w_in = nc.dram_tensor("w_in_b", [L, D, PROJW], BF16).ap()
    w_abc = [nc.dram_tensor("w_%s_b" % c_, [L, 512, D], BF16).ap() for c_ in "abc"]
    w_out = nc.dram_tensor("w_out_b", [L, D, D], BF16).ap()
    w_up = nc.dram_tensor("w_up_b", [L, D, 2 * DFF], BF16).ap()
    w_down = nc.dram_tensor("w_down_b", [L, DFF, D], BF16).ap()
    lamv = din("lamv", [L, 1, 256])
    sublnT = din("sublnT", [L, 128, 1])
    conv_wT = din("conv_wT", [L, 128, 3, 2 * NFF])
    conv_bT = din("conv_bT", [L, 128, 2 * NFF])
    rpbG = din("rpbG", [L, 8, 128, GE * 64])
    gmask = din("gmask", [128, GE * 64], BF16)
    ropeC_d = din("ropeC", [128, NLAT], BF16)
    ropeS_d = din("ropeS", [128, NLAT], BF16)
    Rm_d = din("Rm", [128, 128], BF16)
    CS_d = din("CSm", [128, 256], BF16)
    d256_d = din("dft256", [128, 2, 512], BF16)
    dftN = din("dftN", [16, 128, 2, NLAT], BF16)
    kaug_d = din("kaugc", [32, T], BF16)
    qaug_d = din("qaugc", [32, T], BF16)

    outT = nc.dram_tensor("outT", [NB, 8, 128, NLAT], F32, kind="ExternalOutput").ap()
    hT = nc.dram_tensor("hT_scr", [NB, 8, 128, T], F32).ap()
    Gd = nc.dram_tensor("G_scr", [L, 8, 128, GE * 64], BF16).ap()
    dbg_out = {}
    if dbg:
        for nm, shp, dt in (("d_aT", [128, 8, T], BF16), ("d_fa", [128, 4, T], BF16), ("d_db", [128, 4, T], BF16),
                            ("d_nc", [128, 4, T], BF16), ("d_mg", [128, 8, T], BF16), ("d_mod", [128, 48, 3], F32),
                            ("d_fT", [128, 8, T], BF16), ("d_h1", [8, 128, T], F32)):
            dbg_out[nm] = nc.dram_tensor(nm, shp, dt, kind="ExternalOutput").ap()

    with contextlib.ExitStack() as st:
        S = Sched(nc, st)

        uniq = [0]

        def sbt(stack, name, shape, dt):
            uniq[0] += 1
            return stack.enter_context(nc.sbuf_tensor("%s_%d" % (name, uniq[0]), list(shape), dt))

        def MM(out, lhsT, rhs, start, stop, r, w):
            S.op("pe", lambda e: e.matmul(out, lhsT=lhsT, rhs=rhs, start=start, stop=stop), reads=r, writes=w, inc=True)

        def ACT(out, in_, func, r, w, bias=None, scale=None):
            kw = {}
            if bias is not None:
                kw["bias"] = bias
            if scale is not None:
                kw["scale"] = scale
            S.op("act", lambda e: e.activation(out=out, in_=in_, func=func, **kw), reads=r, writes=w)

        def TT(eng, out, in0, in1, op, r, w):
            S.op(eng, lambda e: e.tensor_tensor(out=out, in0=in0, in1=in1, op=op), reads=r, writes=w)

        def TS(eng, out, in0, s1, s2, op0, op1, r, w):
            if s2 is None:
                S.op(eng, lambda e: e.tensor_scalar(out=out, in0=in0, scalar1=s1, scalar2=None, op0=op0), reads=r, writes=w)
            else:
                S.op(eng, lambda e: e.tensor_scalar(out=out, in0=in0, scalar1=s1, scalar2=s2, op0=op0, op1=op1), reads=r, writes=w)

        def STT(eng, out, in0, scalar, in1, op0, op1, r, w):
            S.op(eng, lambda e: e.scalar_tensor_tensor(out=out, in0=in0, scalar=scalar, in1=in1, op0=op0, op1=op1), reads=r, writes=w)

        def CP(eng, out, in_, r, w):
            if eng == "act":
                S.op("act", lambda e: e.copy(out=out, in_=in_), reads=r, writes=w)
            else:
                S.op(eng, lambda e: e.tensor_copy(out=out, in_=in_), reads=r, writes=w)

        def RCP(out, in_, r, w):
            S.op("dve", lambda e: e.reciprocal(out=out, in_=in_), reads=r, writes=w)

        def MSET(eng, ap, val, w):
            S.op(eng, lambda e: e.memset(ap, val), reads=(), writes=w)

        def DMA(out, in_, r, w, eng="sp"):
            S.dma(lambda e: e.dma_start(out=out, in_=in_), reads=r, writes=w, eng=eng)

        evac_rr = [0]

        def EV(out, in_, r, w):
            evac_rr[0] ^= 1
            CP("act" if evac_rr[0] else "dve", out, in_, r, w)

        aT = sbt(st, "aT", [128, 8, T], BF16)
        mg = sbt(st, "mg", [128, 8 * T], BF16)
        mgv = mg[:].rearrange("p (c t) -> p c t", c=8)
        Zv = mg[:].rearrange("p (i g x) -> p i g x", i=18, g=4)
        uv = mg[:, 0:NFF * 512].rearrange("p (i t) -> p i t", i=NFF)
        br = sbt(st, "br", [128, 4, T], BF16)
        WAR = [sbt(st, "WA", [128, 12288], BF16), sbt(st, "WB", [128, 12288], BF16)]
        ones = sbt(st, "ones", [128, 128], BF16)
        ones_f = sbt(st, "ones_f", [1, 128], F32)
        ones32 = sbt(st, "ones32", [128, 128], F32)
        epsc = sbt(st, "epsc", [128, 2], F32)
        ropeC = sbt(st, "ropeC", [128, NLAT], BF16)
        ropeS = sbt(st, "ropeS", [128, NLAT], BF16)
        Rm = sbt(st, "Rm", [128, 128], BF16)
        CSm = sbt(st, "CSm", [128, 256], BF16)
        d256 = sbt(st, "d256", [128, 2, 512], BF16)
        modT = sbt(st, "modT", [128, L, 48, 3], F32)
        colA1 = sbt(st, "colA1", [128, L, 8, 3], F32)
        colA2 = sbt(st, "colA2", [128, L, 8, 3], F32)
        gfin = sbt(st, "gfin", [128, 8], F32)
        bgate = sbt(st, "bgate", [128, L, 24], F32)
        cw = sbt(st, "cw", [128, L, 3, 2 * NFF], F32)
        cb = sbt(st, "cb", [128, L, 2 * NFF], F32)
        neglam = sbt(st, "neglam", [128, L], F32)
        sgc = sbt(st, "sgc", [128, L], F32)
        PS = [st.enter_context(nc.psum_tensor("ps%d" % i, [128, 512], F32)) for i in range(8)]

        def pk(i):
            return ("ps", i)

        AK = [("aT", t) for t in range(5)]
        MK = [("mg", t) for t in range(5)]
        BK = [("br", t) for t in range(5)]

        def wview(a, k, n):
            return WAR[a][:, 0:k * n].rearrange("p (k n) -> p k n", k=k)

        def wsub(a, off, k, n):
            return WAR[a][:, off:off + k * n].rearrange("p (k n) -> p k n", k=k)

        nphase = [0]

        def phase_end():
            S.barrier()
            S.run()
            nphase[0] += 1

        def stopped():
            return stop is not None and nphase[0] >= stop

        def prologue():
          with contextlib.ExitStack() as ph:
              cT = sbt(ph, "cT", [128, 8, 3], F32)
              scT = sbt(ph, "scT", [128, 8, 3], BF16)
              sgm = sbt(ph, "sgm", [128, 8, 3], F32)
              gmx = sbt(ph, "gmx", [128, L, 8], F32)
              gff = sbt(ph, "gff", [128, L, 8], F32)
              bada = sbt(ph, "bada", [128, L, 48], F32)
              lamr = sbt(ph, "lamr", [1, L, 4, 64], F32)
              lamp = sbt(ph, "lamp", [1, L, 2, 64], F32)
              lams = sbt(ph, "lams", [1, L, 2], F32)
              lam1 = sbt(ph, "lam1", [1, L], F32)
              subl = sbt(ph, "subl", [128, L], F32)
              gf = sbt(ph, "gf", [128, GE * 64], F32)
              gm = sbt(ph, "gm", [128, GE * 64], BF16)
              gb = [sbt(ph, "gb0", [128, GE * 64], BF16), sbt(ph, "gb1", [128, GE * 64], BF16)]

              MSET("dve", ones[:], 1.0, ["ones"])
              MSET("dve", ones_f[:], 1.0, ["ones_f"])
              MSET("dve", ones32[:], 1.0, ["ones32"])
              MSET("dve", epsc[:, 0:1], NORM_EPS, ["epsc"])
              MSET("dve", epsc[:, 1:2], SUBLN_EPS, ["epsc"])
              DMA(ropeC[:], ropeC_d, [], ["ropeC"])
              DMA(ropeS[:], ropeS_d, [], ["ropeS"])
              DMA(Rm[:], Rm_d, [], ["Rm"])
              DMA(CSm[:], CS_d, [], ["CSm"])
              DMA(d256[:], d256_d, [], ["d256"])
              DMA(cT[:], cT_d, [], ["cT"])
              DMA(gfin[:], g_finT, [], ["gfin"])
              DMA(gm[:], gmask, [], ["gm"])
              for l in range(L):
                  DMA(gmx[:, l, :], g_mixT[l], [], ["gmx"])
                  DMA(gff[:, l, :], g_ffnT[l], [], ["gff"])
                  DMA(bada[:, l, :], b_adaT[l], [], ["bada"])
                  DMA(bgate[:, l, :], b_gateT[l], [], ["bgate"])
                  DMA(cw[:, l, :, :], conv_wT[l], [], ["cw"])
                  DMA(cb[:, l, :], conv_bT[l], [], ["cb"])
                  DMA(lamr[:, l, :, :], lamv[l].rearrange("o (a d) -> o a d", a=4), [], ["lamr"])
                  DMA(subl[:, l:l + 1], sublnT[l], [], ["subl"])
              for l in range(L):
                  for (dst_, src_, rows_) in ([(w_in, w_in_f, D), (w_out, w_out_f, D), (w_up, w_up_f, D), (w_down, w_down_f, DFF)]
                                              + [(w_abc[j_], w_abc_f[j_], 512) for j_ in range(3)]):
                      for r0_ in range(0, rows_, 256):
                          DMA(dst_[l, r0_:r0_ + 256, :], src_[l, r0_:r0_ + 256, :], [], [("wbf", l)], eng="pool")
              ACT(sgm[:], cT[:], AF.Sigmoid, ["cT"], ["sgm"])
              TT("dve", scT[:], cT[:], sgm[:], ALU.mult, ["cT", "sgm"], ["scT"])
              for l in range(L):
                  TT("dve", lamp[:, l, :, :], lamr[:, l, 0:4:2, :], lamr[:, l, 1:4:2, :], ALU.mult, ["lamr"], ["lamp"])
                  S.op("dve", lambda e, l=l: e.reduce_sum(out=lams[:, l, :], in_=lamp[:, l, :, :], axis=AX.X), reads=["lamp"], writes=["lams"])
              ACT(lams[:], lams[:], AF.Exp, ["lams"], ["lams"])
              for l in range(L):
                  lam_init = 0.8 - 0.6 * math.exp(-0.3 * l)
                  TT("dve", lam1[:, l:l + 1], lams[:, l, 1:2], lams[:, l, 0:1], ALU.subtract, ["lams"], ["lam1"])
                  TS("dve", lam1[:, l:l + 1], lam1[:, l:l + 1], -lam_init, None, ALU.add, None, ["lam1"], ["lam1"])
                  TS("dve", sgc[:, l:l + 1], subl[:, l:l + 1], 1.0 - lam_init, None, ALU.mult, None, ["subl"], ["sgc"])
              MM(PS[7][:, 0:L], ones_f[:], lam1[:], True, True, ["ones_f", "lam1"], [pk(7)])
              CP("dve", neglam[:], PS[7][:, 0:L], [pk(7)], ["neglam"])
              for l in range(L):
                  for nb in range(8):
                      a = nb % 2
                      wv = wview(a, 8, 768)
                      DMA(wv, w_ada[l, :, nb * 768:(nb + 1) * 768].rearrange("(k p) n -> p k n", p=128), [], [("W", a)], eng="pool")
                      for nn in range(6):
                          j = nb * 6 + nn
                          for k in range(8):
                              MM(PS[l % 2][:, j * 3:(j + 1) * 3], wv[:, k, nn * 128:(nn + 1) * 128], scT[:, k, :], k == 0, k == 7,
                                 [("W", a), "scT"], [pk(l % 2)])
                  for r in range(3):
                      TT("dve", modT[:, l, :, r], PS[l % 2][:, r:144:3], bada[:, l, :], ALU.add, [pk(l % 2), "bada"], ["modT"])
                  for r in range(3):
                      TS("dve", colA1[:, l, :, r], modT[:, l, 8:16, r], 1.0, None, ALU.add, None, ["modT"], ["colA1"])
                      TT("dve", colA1[:, l, :, r], colA1[:, l, :, r], gmx[:, l, :], ALU.mult, ["colA1", "gmx"], ["colA1"])
                      TS("dve", colA2[:, l, :, r], modT[:, l, 32:40, r], 1.0, None, ALU.add, None, ["modT"], ["colA2"])
                      TT("dve", colA2[:, l, :, r], colA2[:, l, :, r], gff[:, l, :], ALU.mult, ["colA2", "gff"], ["colA2"])
              if dbg:
                  DMA(dbg_out["d_mod"], modT[:, 0, :, :], ["modT"], ["d_mod"])
              for l in range(L):
                  for h in range(8):
                      s_ = (l * 8 + h) % 2
                      DMA(gf[:], rpbG[l, h], [], ["gf"])
                      ACT(gf[:], gf[:], AF.Exp, ["gf"], ["gf"])
                      TT("dve", gb[s_][:], gf[:], gm[:], ALU.mult, ["gf", "gm"], [("gb", s_)])
                      DMA(Gd[l, h], gb[s_][:], [("gb", s_)], [("Gd", l, h)])
              phase_end()

        def norm_tile(hs, hkey, n, Acol, Bcol, dst, dkeys, tmp, eps_i, Dn=D, nch=8):
            sq, tf, rs = tmp
            for c in range(nch):
                ACT(sq[c % 2][:, 0:n], hs[:, c, :], AF.Square, [hkey], [("n_sq", c % 2)])
                MM(PS[7][:, 0:n], ones[:], sq[c % 2][:, 0:n], c == 0, c == nch - 1, [("n_sq", c % 2), "ones"], [pk(7)])
            ACT(rs[:, 0:n], PS[7][:, 0:n], AF.Sqrt, [pk(7), "epsc"], ["n_rs"], bias=epsc[:, eps_i:eps_i + 1], scale=1.0 / Dn)
            RCP(rs[:, 0:n], rs[:, 0:n], ["n_rs"], ["n_rs"])
            for c in range(nch):
                TT("dve", tf[c % 2][:, 0:n], hs[:, c, :], rs[:, 0:n], ALU.mult, [hkey, "n_rs"], [("n_tf", c % 2)])
                if Bcol is None:
                    ACT(dst(c), tf[c % 2][:, 0:n], AF.Identity, [("n_tf", c % 2)], dkeys, scale=Acol(c))
                else:
                    ACT(dst(c), tf[c % 2][:, 0:n], AF.Identity, [("n_tf", c % 2)], dkeys, scale=Acol(c), bias=Bcol(c))

        def norm_tmp(ph):
            return ([sbt(ph, "n_sq0", [128, 512], BF16), sbt(ph, "n_sq1", [128, 512], BF16)],
                    [sbt(ph, "n_tf0", [128, 512], F32), sbt(ph, "n_tf1", [128, 512], F32)],
                    sbt(ph, "n_rs", [128, 512], F32))

        def rr_of(b, ti):
            return 2 if ti == 4 else b

        def tiles_of(l):
            return list(range(5)) if l < L - 1 else list(range(4))

        def load_cols(a, off, src, k, n, eng="pool"):
            DMA(wsub(a, off, k, n), src.rearrange("(k p) n -> p k n", p=128), [], [("W", a)], eng=eng)

        def pre_fa(l):
            load_cols(0, 0, w_in[l, :, 0:512], 8, 512)

        def pre_merge(l, j):
            load_cols(1, 0, w_in[l, :, 3584 + j * 1024:3584 + (j + 1) * 1024], 8, 1024)
            load_cols(1, 8192, w_abc[j][l], 4, 1024)

        def pre_qkv(l, base):
            for i in range(3):
                load_cols(0, i * 4096, w_in[l, :, base + i * 512:base + (i + 1) * 512], 8, 512)

        def pre_out(l):
            load_cols(0, 0, w_out[l], 8, 1024)

        def ffn_groups():
            return [(i0, min(4, NFF - i0)) for i0 in range(0, NFF, 4)]

        def pre_up(l, a, i0, gi):
            load_cols(a, 0, w_up[l, :, i0 * 128:(i0 + gi) * 128], 8, gi * 128)
            load_cols(a, 4096, w_up[l, :, DFF + i0 * 128:DFF + (i0 + gi) * 128], 8, gi * 128)

        def phase_n1_first(b, pre):
            if stopped():
                return
            with contextlib.ExitStack() as ph:
                ht = [sbt(ph, "ht0", [128, 8, 512], F32), sbt(ph, "ht1", [128, 8, 512], F32)]
                tmp = norm_tmp(ph)
                pre()
                for ti, (t0, n) in enumerate(TILES):
                    s_ = ti % 2
                    r = rr_of(b, ti)
                    DMA(ht[s_][:, :, 0:n], h0T[b, :, :, t0:t0 + n].rearrange("c p t -> p c t"), [], [("ht", s_)])
                    norm_tile(ht[s_][:, :, 0:n], ("ht", s_), n,
                              lambda c, r=r: colA1[:, 0, c, r:r + 1], lambda c, r=r: modT[:, 0, c, r:r + 1],
                              lambda c, t0=t0, n=n: aT[:, c, t0:t0 + n], [AK[ti]], tmp, 0)
                if dbg and b == 0:
                    DMA(dbg_out["d_aT"], aT[:], AK, ["d_aT"])
                phase_end()

        def phase_fa(b, l, pre):
            if stopped():
                return
            last = l == L - 1
            with contextlib.ExitStack() as ph:
                uT = sbt(ph, "uT", [128, 4, T], BF16)
                dt_ = [sbt(ph, "dft0", [128, 2, 512], BF16), sbt(ph, "dft1", [128, 2, 512], BF16), sbt(ph, "dft2", [128, 2, 512], BF16)]
                pre()
                wfa = wview(0, 8, 512)
                bank = 0
                for ti, (t0, n) in enumerate(TILES):
                    for g in range(4):
                        p = PS[bank % 4]
                        for k in range(8):
                            MM(p[:, 0:n], wfa[:, k, g * 128:(g + 1) * 128], aT[:, k, t0:t0 + n], k == 0, k == 7,
                               [("W", 0), AK[ti]], [pk(bank % 4)])
                        EV(uT[:, g, t0:t0 + n], p[:, 0:n], [pk(bank % 4)], [("uT", ti)])
                        bank += 1
                for i in range(18):
                    ti = min(i // 4, 4)
                    for half in range(2):
                        bi = 4 + (i % 2) * 2 + half
                        for gg in range(2):
                            g = half * 2 + gg
                            MM(PS[bi][:, gg * 256:(gg + 1) * 256], uT[:, g, i * 128:(i + 1) * 128], CSm[:], True, True,
                               [("uT", ti), "CSm"], [pk(bi)])
                        EV(Zv[:, i, half * 2:half * 2 + 2, :], PS[bi][:].rearrange("p (g x) -> p g x", g=2), [pk(bi)], [("Z", i)])
                sc_lat = 1.0 / math.sqrt(NLAT * 128.0)
                cnt = 0
                for j in range(4):
                    for i in range(16):
                        s_ = cnt % 3
                        cnt += 1
                        DMA(dt_[s_][:], dftN[i, :, :, j * 512:(j + 1) * 512], [], [("dft", s_)])
                        for g in range(4):
                            bi = (j % 2) * 4 + g
                            MM(PS[bi][:], Zv[:, i, g, 0:128], dt_[s_][:, 0, :], i == 0, False, [("Z", i), ("dft", s_)], [pk(bi)])
                            MM(PS[bi][:], Zv[:, i, g, 128:256], dt_[s_][:, 1, :], False, i == 15, [("Z", i), ("dft", s_)], [pk(bi)])
                    for g in range(4):
                        bi = (j % 2) * 4 + g
                        if g % 2 == 0:
                            ACT(br[:, g, j * 512:(j + 1) * 512], PS[bi][:], AF.Identity, [pk(bi)], [BK[j]], scale=sc_lat)
                        else:
                            TS("dve", br[:, g, j * 512:(j + 1) * 512], PS[bi][:], sc_lat, None, ALU.mult, None, [pk(bi)], [BK[j]])
                if not last:
                    sc_c = 1.0 / math.sqrt(NCTX * 128.0)
                    for g in range(4):
                        bi = g
                        for i in range(2):
                            MM(PS[bi][:, 0:256], Zv[:, 16 + i, g, 0:128], d256[:, i, 0:256], i == 0, False, [("Z", 16 + i), "d256"], [pk(bi)])
                            MM(PS[bi][:, 0:256], Zv[:, 16 + i, g, 128:256], d256[:, i, 256:512], False, i == 1, [("Z", 16 + i), "d256"], [pk(bi)])
                        TS("dve", br[:, g, NLAT:T], PS[bi][:, 0:256], sc_c, None, ALU.mult, None, [pk(bi)], [BK[4]])
                if dbg and b == 0 and l == 0:
                    DMA(dbg_out["d_fa"], br[:], BK, ["d_fa"])
                phase_end()

        def phase_merge(b, l, j, pre):
            if stopped():
                return
            with contextlib.ExitStack() as ph:
                gate = [sbt(ph, "gate0", [128, 512], F32), sbt(ph, "gate1", [128, 512], F32)]
                tmpm = [sbt(ph, "tmpm0", [128, 512], F32), sbt(ph, "tmpm1", [128, 512], F32)]
                pre()
                wg = wsub(1, 0, 8, 1024)
                wx = wsub(1, 8192, 4, 1024)
                cnt = 0
                for ti in tiles_of(l):
                    t0, n = TILES[ti]
                    for n8 in range(8):
                        s_ = cnt % 2
                        b0 = (cnt % 2) * 2
                        cnt += 1
                        for k in range(8):
                            MM(PS[b0][:, 0:n], wg[:, k, n8 * 128:(n8 + 1) * 128], aT[:, k, t0:t0 + n], k == 0, k == 7,
                               [("W", 1), AK[ti]], [pk(b0)])
                        for g in range(4):
                            MM(PS[b0 + 1][:, 0:n], wx[:, g, n8 * 128:(n8 + 1) * 128], br[:, g, t0:t0 + n], g == 0, g == 3,
                               [("W", 1), BK[ti]], [pk(b0 + 1)])
                        ACT(gate[s_][:, 0:n], PS[b0][:, 0:n], AF.Sigmoid, [pk(b0), "bgate"], [("gate", s_)],
                            bias=bgate[:, l, j * 8 + n8:j * 8 + n8 + 1], scale=1.0)
                        if j == 0:
                            TT("dve", mgv[:, n8, t0:t0 + n], gate[s_][:, 0:n], PS[b0 + 1][:, 0:n], ALU.mult,
                               [("gate", s_), pk(b0 + 1)], [MK[ti]])
                        else:
                            TT("dve", tmpm[s_][:, 0:n], gate[s_][:, 0:n], PS[b0 + 1][:, 0:n], ALU.mult,
                               [("gate", s_), pk(b0 + 1)], [("tmpm", s_)])
                            TT("pool", mgv[:, n8, t0:t0 + n], mgv[:, n8, t0:t0 + n], tmpm[s_][:, 0:n], ALU.add,
                               [("tmpm", s_), MK[ti]], [MK[ti]])
                if dbg and b == 0 and l == 0 and j == 2:
                    DMA(dbg_out["d_mg"], mgv, MK, ["d_mg"])
                phase_end()

        def attn_pipeline(nk, s_fn, e_fn, v_fn):
            for step in range(nk + 2):
                if step < nk:
                    s_fn(step)
                if 1 <= step < nk + 1:
                    e_fn(step - 1)
                if step >= 2:
                    v_fn(step - 2)

        def phase_da(b, l, pre):
            if stopped():
                return
            last = l == L - 1
            with contextlib.ExitStack() as ph:
                qT = sbt(ph, "qT", [128, T], BF16)
                kT = sbt(ph, "kT", [128, T], BF16)
                Vt = sbt(ph, "Vt", [128, 18, 128], BF16)
                xb = [sbt(ph, "xb0", [128, 512], BF16), sbt(ph, "xb1", [128, 512], BF16)]
                t1 = [sbt(ph, "t1_0", [128, 512], F32), sbt(ph, "t1_1", [128, 512], F32)]
                t2 = [sbt(ph, "t2_0", [128, 512], F32), sbt(ph, "t2_1", [128, 512], F32)]
                Pb = [sbt(ph, "P%d" % i, [128, 512], BF16) for i in range(3)]
                r0 = sbt(ph, "r0", [128, 512], F32)
                r1 = sbt(ph, "r1", [128, 512], F32)
                o0 = sbt(ph, "o0", [128, 512], F32)
                o1 = sbt(ph, "o1", [128, 512], F32)
                sqo = sbt(ph, "sqo", [128, 512], BF16)
                rso = sbt(ph, "rso", [128, 512], F32)
                Pac = [[sbt(ph, "Pac%d%d" % (m_, e_), [128, 512], F32) for e_ in range(2)] for m_ in range(2)]
                pre()
                wq = wsub(0, 0, 8, 512)
                wk = wsub(0, 4096, 8, 512)
                wv = wsub(0, 8192, 8, 512)
                qtiles = tiles_of(l)
                rc = [0]

                def proj_rope(w, dstT, dkey, h, tlist):
                    pend_ = []
                    for ti in tlist:
                        t0, n = TILES[ti]
                        s_ = rc[0] % 2
                        rc[0] += 1
                        pb_ = s_ * 2
                        for k in range(8):
                            MM(PS[pb_][:, 0:n], w[:, k, h * 128:(h + 1) * 128], aT[:, k, t0:t0 + n], k == 0, k == 7,
                               [("W", 0), AK[ti]], [pk(pb_)])
                        if ti < 4:
                            CP("act", xb[s_][:, 0:n], PS[pb_][:, 0:n], [pk(pb_)], [("xb", s_)])

                            def rope(s_=s_, pb_=pb_, t0=t0, n=n, ti=ti):
                                MM(PS[pb_ + 1][:, 0:n], Rm[:], xb[s_][:, 0:n], True, True, ["Rm", ("xb", s_)], [pk(pb_ + 1)])
                                TT("dve", t1[s_][:, 0:n], xb[s_][:, 0:n], ropeC[:, t0:t0 + n], ALU.mult, [("xb", s_), "ropeC"], [("t1", s_)])
                                TT("dve", t2[s_][:, 0:n], PS[pb_ + 1][:, 0:n], ropeS[:, t0:t0 + n], ALU.mult, [pk(pb_ + 1), "ropeS"], [("t2", s_)])
                                TT("pool", dstT[:, t0:t0 + n], t1[s_][:, 0:n], t2[s_][:, 0:n], ALU.add, [("t1", s_), ("t2", s_)], [(dkey, ti)])

                            if pend_:
                                pend_.pop()()
                            pend_.append(rope)
                        else:
                            CP("act", dstT[:, t0:t0 + n], PS[pb_][:, 0:n], [pk(pb_)], [(dkey, ti)])
                    if pend_:
                        pend_.pop()()

                for h in range(4):
                    proj_rope(wk, kT, "kT", h, list(range(5)))
                    for i4 in range(5):
                        ni = 4 if i4 < 4 else 2
                        bi = 4 + (i4 % 2)
                        for ii in range(ni):
                            i = i4 * 4 + ii
                            ti = min(i // 4, 4)
                            for k in range(8):
                                MM(PS[bi][:, ii * 128:(ii + 1) * 128], aT[:, k, i * 128:(i + 1) * 128], wv[:, k, h * 128:(h + 1) * 128],
                                   k == 0, k == 7, [("W", 0), AK[ti]], [pk(bi)])
                        EV(Vt[:, i4 * 4:i4 * 4 + ni, :], PS[bi][:, 0:ni * 128].rearrange("p (i e) -> p i e", i=ni), [pk(bi)], [("Vt", i4)])
                    proj_rope(wq, qT, "qT", h, qtiles)
                    steps = []
                    for ti in qtiles:
                        keys = list(range(18)) if ti < 4 else [16, 17]
                        for m in range(2):
                            for sidx, i in enumerate(keys):
                                steps.append((ti, m, sidx, len(keys), i))
                    NS = len(steps)

                    def s_fn(g):
                        ti, m, sidx, nk, i = steps[g]
                        t0, n = TILES[ti]
                        kti = min(i // 4, 4)
                        MM(PS[g % 3][:, 0:n], kT[m * 64:(m + 1) * 64, i * 128:(i + 1) * 128], qT[m * 64:(m + 1) * 64, t0:t0 + n],
                           True, True, [("kT", kti), ("qT", ti)], [pk(g % 3)])

                    def e_fn(g):
                        ti, m, sidx, nk, i = steps[g]
                        n = TILES[ti][1]
                        ACT(Pb[g % 3][:, 0:n], PS[g % 3][:, 0:n], AF.Exp, [pk(g % 3)], [("P", g % 3)], scale=0.125)

                    def v_fn(g):
                        ti, m, sidx, nk, i = steps[g]
                        n = TILES[ti][1]
                        pa = PS[3 + m]
                        pd = PS[5 + m]
                        MM(pa[:, 0:n], Vt[:, i, :], Pb[g % 3][:, 0:n], sidx == 0, sidx == nk - 1,
                           [("Vt", i // 4), ("P", g % 3)], [pk(3 + m)])
                        e_ = sidx % 2
                        eng_ = "pool" if e_ == 0 else "dve"
                        if sidx < 2:
                            CP(eng_, Pac[m][e_][:, 0:n], Pb[g % 3][:, 0:n], [("P", g % 3)], [("Pac", m, e_)])
                        else:
                            TT(eng_, Pac[m][e_][:, 0:n], Pac[m][e_][:, 0:n], Pb[g % 3][:, 0:n], ALU.add,
                               [("P", g % 3), ("Pac", m, e_)], [("Pac", m, e_)])
                        if sidx == nk - 1:
                            MM(pd[:, 0:n], ones32[:], Pac[m][0][:, 0:n], True, False, ["ones32", ("Pac", m, 0)], [pk(5 + m)])
                            MM(pd[:, 0:n], ones32[:], Pac[m][1][:, 0:n], False, True, ["ones32", ("Pac", m, 1)], [pk(5 + m)])

                    def fin1(ti, h=h):
                        t0, n = TILES[ti]
                        RCP(r0[:, 0:n], PS[5][:, 0:n], [pk(5)], ["r0"])
                        RCP(r1[:, 0:n], PS[6][:, 0:n], [pk(6)], ["r1"])
                        TT("dve", o0[:, 0:n], PS[3][:, 0:n], r0[:, 0:n], ALU.mult, [pk(3), "r0"], ["o0"])
                        TT("dve", o1[:, 0:n], PS[4][:, 0:n], r1[:, 0:n], ALU.mult, [pk(4), "r1"], ["o1"])
                        STT("dve", o0[:, 0:n], o1[:, 0:n], neglam[:, l:l + 1], o0[:, 0:n], ALU.mult, ALU.add, ["o0", "o1", "neglam"], ["o0"])
                        ACT(sqo[:, 0:n], o0[:, 0:n], AF.Square, ["o0"], ["sqo"])

                    def fin2(ti, h=h):
                        t0, n = TILES[ti]
                        MM(PS[7][:, 0:n], ones[:], sqo[:, 0:n], True, True, ["ones", "sqo"], [pk(7)])
                        ACT(rso[:, 0:n], PS[7][:, 0:n], AF.Sqrt, [pk(7), "epsc"], ["rso"], bias=epsc[:, 1:2], scale=1.0 / 128.0)
                        RCP(rso[:, 0:n], rso[:, 0:n], ["rso"], ["rso"])
                        TT("dve", o0[:, 0:n], o0[:, 0:n], rso[:, 0:n], ALU.mult, ["o0", "rso"], ["o0"])
                        ACT(br[:, h, t0:t0 + n], o0[:, 0:n], AF.Identity, ["o0", "sgc"], [BK[ti]], scale=sgc[:, l:l + 1])

                    later = []
                    for step in range(NS + 2):
                        if step < NS:
                            s_fn(step)
                        if 1 <= step < NS + 1:
                            e_fn(step - 1)
                        if step >= 2:
                            g = step - 2
                            v_fn(g)
                            ti, m, sidx, nk, i = steps[g]
                            if m == 1 and sidx == nk - 1:
                                while later:
                                    fin2(later.pop(0)[1])
                                fin1(ti)
                                later.append((step + 6, ti))
                        while later and later[0][0] <= step:
                            fin2(later.pop(0)[1])
                    while later:
                        fin2(later.pop(0)[1])
                if dbg and b == 0 and l == 0:
                    DMA(dbg_out["d_db"], br[:], BK, ["d_db"])
                phase_end()

        def phase_na(b, l, pre):
            if stopped():
                return
            last = l == L - 1
            with contextlib.ExitStack() as ph:
                qa = sbt(ph, "qa", [96, T], BF16)
                ka = sbt(ph, "ka", [96, T], BF16)
                Vt = sbt(ph, "Vtn", [128, 18, 512], BF16)
                Gt = [sbt(ph, "Gt0", [128, GE * 64], BF16), sbt(ph, "Gt1", [128, GE * 64], BF16)]
                Pb = [sbt(ph, "Pn%d" % i, [128, 512], BF16) for i in range(3)]
                rn = [sbt(ph, "rn0", [64, 512], F32), sbt(ph, "rn1", [64, 512], F32)]
                Vh = [sbt(ph, "Vh%d" % i, [128, 18, 128], BF16) for i in range(2)]
                pre()
                wq = wsub(0, 0, 8, 512)
                wk = wsub(0, 4096, 8, 512)
                wv = wsub(0, 8192, 8, 512)
                qtiles = tiles_of(l)
                DMA(qa[64:96, :], qaug_d, [], ["qa_aug"])
                DMA(ka[64:96, :], kaug_d, [], ["ka_aug"])
                for i_ in range(2):
                    MSET("pool", Vh[i_][:, :, 64:128], 1.0, [("Vh", i_)])
                for i in range(18):
                    ti = min(i // 4, 4)
                    bi = 4 + i % 2
                    for k in range(8):
                        MM(PS[bi][:], aT[:, k, i * 128:(i + 1) * 128], wv[:, k, :], k == 0, k == 7, [("W", 0), AK[ti]], [pk(bi)])
                    EV(Vt[:, i, :], PS[bi][:], [pk(bi)], [("Vt", i)])
                rc = 0
                for h in range(8):
                    gs = h % 2
                    DMA(Gt[gs][:], Gd[l, h], [("Gd", l, h)], [("Gt", gs)])
                    CP("pool", Vh[gs][:, :, 0:64], Vt[:, :, h * 64:(h + 1) * 64], [("Vt", i_) for i_ in range(18)], [("Vh", gs)])
                    for (w, dst, dkey, tl) in ((wk, ka, "ka", list(range(5))), (wq, qa, "qa", qtiles)):
                        for ti in tl:
                            t0, n = TILES[ti]
                            pb_ = 6 + rc % 2
                            rc += 1
                            for k in range(8):
                                MM(PS[pb_][0:64, 0:n], w[:, k, h * 64:(h + 1) * 64], aT[:, k, t0:t0 + n], k == 0, k == 7,
                                   [("W", 0), AK[ti]], [pk(pb_)])
                            EV(dst[0:64, t0:t0 + n], PS[pb_][0:64, 0:n], [pk(pb_)], [(dkey, ti)])
                    hp = (h % 2) * 64
                    steps = []
                    for tix, ti in enumerate(qtiles):
                        if ti < 4:
                            rb = 8 * ti
                            kr_start, nlat = (0, 6) if ti == 0 else ((20, 6) if ti == 3 else (rb - 4, 8))
                            keys = [(kr_start // 2 + j_, 14 - (kr_start + 2 * j_) + rb) for j_ in range(nlat)] + [(16, None), (17, None)]
                        else:
                            keys = [(16, None), (17, None)]
                        for sidx, (i, e0) in enumerate(keys):
                            steps.append((ti, sidx, len(keys), i, e0, 3 if tix % 2 == 0 else 5))
                    NS = len(steps)

                    def s_fn(g):
                        ti, sidx, nk, i, e0, ab = steps[g]
                        t0, n = TILES[ti]
                        kti = min(i // 4, 4)
                        MM(PS[g % 3][:, 0:n], ka[0:96, i * 128:(i + 1) * 128], qa[0:96, t0:t0 + n], True, True,
                           [("ka", kti), "ka_aug", ("qa", ti), "qa_aug"], [pk(g % 3)])

                    def e_fn(g, gs=gs):
                        ti, sidx, nk, i, e0, ab = steps[g]
                        n = TILES[ti][1]
                        ACT(Pb[g % 3][:, 0:n], PS[g % 3][:, 0:n], AF.Exp, [pk(g % 3)], [("P", g % 3)], scale=0.125)
                        if e0 is not None:
                            TT("dve", Pb[g % 3][:, 0:n], Pb[g % 3][:, 0:n], Gt[gs][:, e0 * 64:e0 * 64 + n], ALU.mult,
                               [("P", g % 3), ("Gt", gs)], [("P", g % 3)])

                    def v_fn(g, gs=gs, h=h, hp=hp):
                        ti, sidx, nk, i, e0, ab = steps[g]
                        t0, n = TILES[ti]
                        MM(PS[ab][:, 0:n], Vh[gs][:, i, :], Pb[g % 3][:, 0:n], sidx == 0, sidx == nk - 1,
                           [("Vh", gs), ("P", g % 3)], [pk(ab)])
                        if sidx == nk - 1:
                            rn_ = rn[0] if ab == 3 else rn[1]
                            rk_ = "rn%d" % ab
                            RCP(rn_[:, 0:n], PS[ab][64:128, 0:n], [pk(ab)], [rk_])
                            TT("dve", br[hp:hp + 64, h // 2, t0:t0 + n], PS[ab][0:64, 0:n], rn_[:, 0:n], ALU.mult, [pk(ab), rk_], [BK[ti]])

                    attn_pipeline(NS, s_fn, e_fn, v_fn)
                if dbg and b == 0 and l == 0:
                    DMA(dbg_out["d_nc"], br[:], BK, ["d_nc"])
                phase_end()

        def phase_out(b, l, pre):
            if stopped():
                return
            with contextlib.ExitStack() as ph:
                ht = [sbt(ph, "ht0", [128, 8, 512], F32), sbt(ph, "ht1", [128, 8, 512], F32)]
                tmp = norm_tmp(ph)
                pre()
                wo = wsub(0, 0, 8, 1024)
                hsrc = h0T if l == 0 else hT
                cnt = 0
                for ti in tiles_of(l):
                    t0, n = TILES[ti]
                    s_ = ti % 2
                    r = rr_of(b, ti)
                    rk = [] if l == 0 else [("hT", ti)]
                    DMA(ht[s_][:, :, 0:n], hsrc[b, :, :, t0:t0 + n].rearrange("c p t -> p c t"), rk, [("ht", s_)])
                    for n8 in range(8):
                        bi = cnt % 4
                        cnt += 1
                        for k in range(8):
                            MM(PS[bi][:, 0:n], wo[:, k, n8 * 128:(n8 + 1) * 128], mgv[:, k, t0:t0 + n], k == 0, k == 7,
                               [("W", 0), MK[ti]], [pk(bi)])
                        STT("dve", ht[s_][:, n8, 0:n], PS[bi][:, 0:n], modT[:, l, 16 + n8, r:r + 1], ht[s_][:, n8, 0:n], ALU.mult, ALU.add,
                            [pk(bi), ("ht", s_), "modT"], [("ht", s_)])
                    DMA(hT[b, :, :, t0:t0 + n].rearrange("c p t -> p c t"), ht[s_][:, :, 0:n], [("ht", s_)], [("hT", ti)])
                    norm_tile(ht[s_][:, :, 0:n], ("ht", s_), n,
                              lambda c, r=r: colA2[:, l, c, r:r + 1], lambda c, r=r: modT[:, l, 24 + c, r:r + 1],
                              lambda c, t0=t0, n=n: aT[:, c, t0:t0 + n], [AK[ti]], tmp, 0)
                if dbg and b == 0 and l == 0:
                    DMA(dbg_out["d_fT"], aT[:], AK, ["d_fT"])
                phase_end()

        def phase_ffn(b, l, pre_next):
            if stopped():
                return
            last = l == L - 1
            groups = ffn_groups()
            with contextlib.ExitStack() as ph:
                ht = sbt(ph, "htf", [128, 8, 512], F32)
                tmp = norm_tmp(ph)
                hb = sbt(ph, "hb", [128, 8, 8], BF16)
                hpre = sbt(ph, "hpre", [128, 2 * NFF, 6], F32)
                pa_ = [sbt(ph, "pa0", [128, 514], F32), sbt(ph, "pa1", [128, 514], F32)]
                pb_ = [sbt(ph, "pb0", [128, 514], F32), sbt(ph, "pb1", [128, 514], F32)]
                ya = [sbt(ph, "ya0", [128, 512], F32), sbt(ph, "ya1", [128, 512], F32)]
                yb = [sbt(ph, "yb0", [128, 512], F32), sbt(ph, "yb1", [128, 512], F32)]
                wdn = [sbt(ph, "wdn%d" % i, [128, 4, 512], BF16) for i in range(2)]
                tl = tiles_of(l)
                for c_, col in enumerate((511, 512, 1023, 1024, 1535, 1536)):
                    CP("dve", hb[:, :, c_:c_ + 1], aT[:, :, col:col + 1], [AK[col // 512]], ["hb"])
                gcount = 0
                dcount = 0
                icount = 0
                for tidx, ti in enumerate(tl):
                    t0, n = TILES[ti]
                    r = rr_of(b, ti)
                    hl = (t0 // 512) * 2 - 2 if t0 not in (0, NLAT) else None
                    hr = (t0 // 512) * 2 + 1 if (t0 + n) not in (NLAT, T) else None
                    DMA(ht[:, :, 0:n], hT[b, :, :, t0:t0 + n].rearrange("c p t -> p c t"), [("hT", ti)], ["htf"])
                    pend = []
                    for gidx, (i0, gi) in enumerate(groups):
                        a = (gcount + 1) % 2
                        gcount += 1
                        if gidx + 1 < len(groups):
                            pre_up(l, (gcount + 1) % 2, *groups[gidx + 1])
                        elif tidx + 1 < len(tl):
                            pre_up(l, (gcount + 1) % 2, *groups[0])
                        wa = wsub(a, 0, 8, gi * 128)
                        wb = wsub(a, 4096, 8, gi * 128)
                        for ii in range(gi):
                            i = i0 + ii
                            s_ = icount % 2
                            icount += 1
                            for (wsel, bi, pre_t, pkey) in ((wa, 4, pa_[s_], ("pa", s_)), (wb, 5, pb_[s_], ("pb", s_))):
                                for k in range(8):
                                    MM(PS[bi][:, 0:n], wsel[:, k, ii * 128:(ii + 1) * 128], aT[:, k, t0:t0 + n], k == 0, k == 7,
                                       [("W", a), AK[ti]], [pk(bi)])
                                CP("act", pre_t[:, 1:n + 1], PS[bi][:, 0:n], [pk(bi)], [pkey])
                            if tidx == 0:
                                for (wsel, j_, co) in ((wa, i, s_ * 12), (wb, NFF + i, s_ * 12 + 6)):
                                    for k in range(8):
                                        MM(PS[6][:, co:co + 6], wsel[:, k, ii * 128:(ii + 1) * 128], hb[:, k, 0:6], k == 0, k == 7,
                                           [("W", a), "hb"], [pk(6)])
                                for (j_, co) in ((i, s_ * 12), (NFF + i, s_ * 12 + 6)):
                                    CP("dve", hpre[:, j_, :], PS[6][:, co:co + 6], [pk(6)], [("hpre", j_)])
                            for (pre_t, pkey, j_) in ((pa_[s_], ("pa", s_), i), (pb_[s_], ("pb", s_), NFF + i)):
                                for (hx, dcol) in ((hl, 0), (hr, n + 1)):
                                    if hx is None:
                                        MSET("pool", pre_t[:, dcol:dcol + 1], 0.0, [pkey])
                                    else:
                                        CP("pool", pre_t[:, dcol:dcol + 1], hpre[:, j_, hx:hx + 1], [("hpre", j_)], [pkey])
                            def chain(s_=s_, i=i, n=n):
                                for (pre_t, pkey, y_, ykey, j_) in ((pa_[s_], ("pa", s_), ya[s_], ("ya", s_), i), (pb_[s_], ("pb", s_), yb[s_], ("yb", s_), NFF + i)):
                                    ACT(y_[:, 0:n], pre_t[:, 1:n + 1], AF.Identity, [pkey, "cw", "cb"], [ykey], scale=cw[:, l, 1, j_:j_ + 1], bias=cb[:, l, j_:j_ + 1])
                                    STT("dve", y_[:, 0:n], pre_t[:, 0:n], cw[:, l, 0, j_:j_ + 1], y_[:, 0:n], ALU.mult, ALU.add, [pkey, ykey, "cw"], [ykey])
                                    STT("dve", y_[:, 0:n], pre_t[:, 2:n + 2], cw[:, l, 2, j_:j_ + 1], y_[:, 0:n], ALU.mult, ALU.add, [pkey, ykey, "cw"], [ykey])
                                ACT(ya[s_][:, 0:n], ya[s_][:, 0:n], AF.Silu, [("ya", s_)], [("ya", s_)])
                                TT("pool", uv[:, i, 0:n], ya[s_][:, 0:n], yb[s_][:, 0:n], ALU.mult, [("ya", s_), ("yb", s_)], [("u", i)])

                            if pend:
                                pend.pop()()
                            pend.append(chain)
                    if pend:
                        pend.pop()()
                    if tidx + 1 == len(tl):
                        pre_next()
                    for half in range(2):
                        for gidx, (i0, gi) in enumerate(groups):
                            ds_ = dcount % 2
                            dcount += 1
                            DMA(wdn[ds_][:, 0:gi, :], w_down[l, i0 * 128:(i0 + gi) * 128, half * 512:(half + 1) * 512].rearrange("(i p) n -> p i n", p=128),
                                [], [("wdn", ds_)], eng="pool")
                            for ii in range(gi):
                                i = i0 + ii
                                for nn in range(4):
                                    MM(PS[nn][:, 0:n], wdn[ds_][:, ii, nn * 128:(nn + 1) * 128], uv[:, i, 0:n], i == 0, i == NFF - 1,
                                       [("wdn", ds_), ("u", i)], [pk(nn)])
                        for nn in range(4):
                            n8 = half * 4 + nn
                            STT("dve", ht[:, n8, 0:n], PS[nn][:, 0:n], modT[:, l, 40 + n8, r:r + 1], ht[:, n8, 0:n], ALU.mult, ALU.add,
                                [pk(nn), "htf", "modT"], ["htf"])
                    if not last:
                        DMA(hT[b, :, :, t0:t0 + n].rearrange("c p t -> p c t"), ht[:, :, 0:n], ["htf"], [("hT", ti)])
                        norm_tile(ht[:, :, 0:n], "htf", n,
                                  lambda c, r=r: colA1[:, l + 1, c, r:r + 1], lambda c, r=r: modT[:, l + 1, c, r:r + 1],
                                  lambda c, t0=t0, n=n: aT[:, c, t0:t0 + n], [AK[ti]], tmp, 0)
                    else:
                        norm_tile(ht[:, :, 0:n], "htf", n, lambda c: gfin[:, c:c + 1], None,
                                  lambda c, n=n: ht[:, c, 0:n], ["htf"], tmp, 0)
                        DMA(outT[b, :, :, t0:t0 + n].rearrange("c p t -> p c t"), ht[:, :, 0:n], ["htf"], [("out", b, ti)])
                if dbg and b == 0 and l == 0 and not last:
                    pass
                phase_end()

        def nop():
            pass

        try:
          prologue()
          for b in range(NB):
            phase_n1_first(b, lambda: pre_fa(0))
            for l in range(L):
                phase_fa(b, l, lambda l=l: pre_merge(l, 0))
                phase_merge(b, l, 0, lambda l=l: pre_qkv(l, 512))
                phase_da(b, l, lambda l=l: pre_merge(l, 1))
                phase_merge(b, l, 1, lambda l=l: pre_qkv(l, 2048))
                phase_na(b, l, lambda l=l: pre_merge(l, 2))
                phase_merge(b, l, 2, lambda l=l: pre_out(l))
                phase_out(b, l, lambda l=l: pre_up(l, 1, *ffn_groups()[0]))
                if l + 1 < L:
                    phase_ffn(b, l, lambda l=l: pre_fa(l + 1))
                else:
                    phase_ffn(b, l, nop)
        except _StopBuild:
            pass
        S.barrier()
        S.run()
    build_program.stats = (S.n_ins, S.n_wait)
    return nc


def _consts():
    bf = ml_dtypes.bfloat16
    c = {}
    t = np.arange(NLAT)
    pos = np.stack([(t // 64).astype(np.float32), (t % 64).astype(np.float32)], 0)
    inv = (10000.0 ** (-np.arange(16, dtype=np.float32) / 16)).astype(np.float32)
    p = np.arange(128)
    axis = (p % 64) // 32
    half = (p % 32) // 16
    f = p % 16
    ang = pos[axis, :] * inv[f][:, None]
    c["ropeC"] = np.cos(ang).astype(bf)
    c["ropeS"] = np.sin(ang).astype(bf)
    R = np.zeros((128, 128), np.float32)
    for q in range(128):
        if half[q] == 0:
            R[q + 16, q] = -1.0
        else:
            R[q - 16, q] = 1.0
    c["Rm"] = R.astype(bf)
    cc = np.arange(128)
    phi = 2 * np.pi * np.outer(cc, cc) / 128.0
    c["CSm"] = np.concatenate([np.cos(phi), np.sin(phi)], 1).astype(bf)
    n2 = np.arange(NCTX)
    th = 2 * np.pi * (np.outer(n2, n2) % NCTX) / NCTX
    d256 = np.concatenate([np.cos(th), -np.sin(th)], 1).reshape(2, 128, 512).transpose(1, 0, 2)
    c["dft256"] = np.ascontiguousarray(d256).astype(bf)
    nn = np.arange(NLAT)
    thn = 2 * np.pi * ((np.outer(nn, nn) % NLAT).astype(np.float64)) / NLAT
    dN = np.stack([np.cos(thn), -np.sin(thn)], 1)
    c["dftN"] = np.ascontiguousarray(dN.reshape(16, 128, 2, NLAT)).astype(bf)
    ka = np.zeros((32, T), np.float32)
    qa = np.zeros((32, T), np.float32)
    for tok in range(NLAT):
        r = tok // 64
        ka[r, tok] = 1.0
        r0 = min(max(r - 4, 0), 24)
        qa[:, tok] = MASKV
        qa[r0:r0 + 8, tok] = 0.0
    c["kaugc"] = ka.astype(bf)
    c["qaugc"] = qa.astype(bf)
    dr = np.zeros((128, GE, 64), np.int64)
    dc = np.zeros((128, GE, 64), np.int64)
    ok = np.zeros((128, GE, 64), bool)
    for e in range(GE):
        d = 21 - e
        for pp in range(128):
            drr = d + (1 if pp >= 64 else 0)
            kc = pp % 64
            for qc in range(64):
                c0 = min(max(qc - 8, 0), 48)
                v = (0 <= drr <= 14) and (c0 <= kc < c0 + 16)
                ok[pp, e, qc] = v
                if v:
                    dr[pp, e, qc] = drr
                    dc[pp, e, qc] = kc - qc + 15
    c["_dr"] = dr.reshape(128, GE * 64)
    c["_dc"] = dc.reshape(128, GE * 64)
    c["gmask"] = ok.reshape(128, GE * 64).astype(np.float32).astype(bf)
    return c


_CONST_CACHE = {}


def _prep_inputs(inp, L, NB, ncores):
    if "c" not in _CONST_CACHE:
        _CONST_CACHE["c"] = _consts()
    C = _CONST_CACHE["c"]
    f32 = np.float32
    shared = {}
    for k in ("w_ada", "w_in", "w_a", "w_b", "w_c", "w_out", "w_up", "w_down"):
        shared[k] = np.ascontiguousarray(np.asarray(inp[k], f32)[:L])

    def fm(a, nch):
        a = np.asarray(a, f32)[:L]
        return np.ascontiguousarray(a.reshape(L, nch, 128).transpose(0, 2, 1))

    shared["b_adaT"] = fm(inp["b_ada"], 48)
    shared["g_mixT"] = fm(inp["g_mix"], 8)
    shared["g_ffnT"] = fm(inp["g_ffn"], 8)
    shared["g_finT"] = np.ascontiguousarray(np.asarray(inp["g_final"], f32).reshape(8, 128).T)
    shared["b_gateT"] = fm(inp["b_gate"], 24)
    shared["lamv"] = np.ascontiguousarray(np.asarray(inp["lam"], f32)[:L].reshape(L, 1, 256))
    shared["sublnT"] = np.ascontiguousarray(np.asarray(inp["subln_g"], f32)[:L].reshape(L, 128, 1))
    cwt = np.asarray(inp["conv_w"], f32)[:L].reshape(L, 3, 2 * NFF, 128).transpose(0, 3, 1, 2)
    shared["conv_wT"] = np.ascontiguousarray(cwt)
    shared["conv_bT"] = fm(inp["conv_b"], 2 * NFF)
    rpb = np.asarray(inp["rpb"], f32)[:L]
    shared["rpbG"] = np.ascontiguousarray(rpb[:, :, C["_dr"], C["_dc"]])
    for k in ("gmask", "ropeC", "ropeS", "Rm", "CSm", "dft256", "dftN", "kaugc", "qaugc"):
        shared[k] = C[k]
    x = np.asarray(inp["x"], f32)
    ctx = np.asarray(inp["ctx"], f32)
    c = np.asarray(inp["c"], f32)
    cc = np.asarray(inp["c_ctx"], f32)
    maps = []
    for core in range(ncores):
        m = dict(shared)
        hs = []
        for j in range(NB):
            bi = core * NB + j
            hcat = np.concatenate([x[bi], ctx[bi]], 0)
            hs.append(hcat.T.reshape(8, 128, T))
        m["h0T"] = np.ascontiguousarray(np.stack(hs, 0))
        cs = np.stack([c[core * NB + j] for j in range(NB)] + [cc], 0)
        m["cT"] = np.ascontiguousarray(cs.reshape(3, 8, 128).transpose(2, 1, 0))
        maps.append(m)
    return maps


_PROG = {}


def kernel(**inputs):
    L, NB, ncores = DEPTH, 2, 8
    if "nc" not in _PROG:
        _PROG["nc"] = build_program(L, NB)
    nc = _PROG["nc"]
    maps = _prep_inputs(inputs, L, NB, ncores)
    res = run_bass_kernel_spmd(nc, maps, core_ids=list(range(ncores)))
    outs = []
    for core in range(ncores):
        o = res.results[core]["outT"]
        for j in range(NB):
            outs.append(np.asarray(o[j], np.float32).reshape(D, NLAT).T)
    return np.ascontiguousarray(np.stack(outs, 0))
```

```python
import contextlib
import math
import numpy as np
import ml_dtypes
import concourse.bass as bass
import concourse.mybir as mybir
from concourse.bass_utils import run_bass_kernel_spmd

F32 = mybir.dt.float32
BF16 = mybir.dt.bfloat16
AF = mybir.ActivationFunctionType
ALU = mybir.AluOpType
AX = mybir.AxisListType

N_DSEM = 8

D = 1024
NLAT = 2048
NCTX = 256
T = NLAT + NCTX
DEPTH = 4
PROJW = 6656
DFF = 2816
NFF = 22
NORM_EPS = 1e-6
SUBLN_EPS = 1e-5
TILES = [(0, 512), (512, 512), (1024, 512), (1536, 512), (2048, 256)]
GE = 30
MASKV = -30000.0


class Sched:
    ENG = ("pe", "act", "dve", "pool", "sp")

    def __init__(self, nc, stack):
        self.nc = nc
        self.q = {e: [] for e in self.ENG}
        self.sem = {e: stack.enter_context(nc.semaphore("s_" + e)) for e in self.ENG}
        self.cnt = {e: 0 for e in self.ENG}
        self.dsem = {}
        self.dcnt = {}
        self.drr = {}
        for e in ("sp", "pool"):
            self.dsem[e] = [stack.enter_context(nc.semaphore("d_%s%d" % (e, i))) for i in range(N_DSEM)]
            self.dcnt[e] = [0] * N_DSEM
            self.drr[e] = 0
        self.seen = {e: {} for e in self.ENG}
        self.state = {}
        self.n_wait = 0
        self.n_ins = 0
        self.pe_dirty = False

    def _semobj(self, k):
        return self.sem[k[1]] if k[0] == "c" else self.dsem[k[1]][k[2]]

    def _deps(self, reads, writes):
        deps = {}

        def add(ev):
            if ev is None:
                return
            k, v = ev
            if deps.get(k, -1) < v:
                deps[k] = v

        for b in reads:
            st = self.state.get(b)
            if st:
                add(st[0])
        for b in writes:
            st = self.state.get(b)
            if st:
                add(st[0])
                for k, v in st[1].items():
                    add((k, v))
        return deps

    def _record(self, ev, reads, writes):
        for b in reads:
            st = self.state.setdefault(b, [None, {}])
            k, v = ev
            if st[1].get(k, -1) < v:
                st[1][k] = v
        for b in writes:
            self.state[b] = [ev, {}]

    def _emit_waits(self, eng, deps, skip_self):
        waits = []
        seen = self.seen[eng]
        for k, v in deps.items():
            if skip_self and k == ("c", eng):
                continue
            if seen.get(k, -1) >= v:
                continue
            seen[k] = v
            waits.append((self._semobj(k), v))
        return waits

    def op(self, eng, fn, reads=(), writes=(), inc=True):
        deps = self._deps(reads, writes)
        waits = self._emit_waits(eng, deps, skip_self=(eng == "pe"))
        if eng == "pe":
            self.pe_dirty = not inc
        if inc:
            self.cnt[eng] += 1
            ev = (("c", eng), self.cnt[eng])
        else:
            ev = (("c", eng), self.cnt[eng] + 1)
        sem = self.sem[eng]
        self.n_wait += len(waits)
        self.n_ins += 1

        def emit(e, fn=fn, waits=waits, inc=inc, sem=sem):
            for s, v in waits:
                e.wait_ge(s, v)
            ins = fn(e)
            if inc:
                ins.then_inc(sem, 1)

        self.q[eng].append(emit)
        self._record(ev, reads, writes)

    def dma(self, fn, reads=(), writes=(), eng="sp"):
        deps = self._deps(reads, writes)
        i = self.drr[eng]
        self.drr[eng] = (i + 1) % N_DSEM
        k = ("d", eng, i)
        prev = self.dcnt[eng][i]
        if prev > 0 and deps.get(k, -1) < prev:
            deps[k] = prev
        waits = self._emit_waits(eng, deps, skip_self=False)
        self.dcnt[eng][i] = prev + 16
        ev = (k, prev + 16)
        sem = self.dsem[eng][i]
        self.n_wait += len(waits)
        self.n_ins += 1

        def emit(e, fn=fn, waits=waits, sem=sem):
            for s, v in waits:
                e.wait_ge(s, v)
            fn(e).then_inc(sem, 16)

        self.q[eng].append(emit)
        self._record(ev, reads, writes)

    def barrier(self):
        assert not self.pe_dirty, "barrier with un-evented PE op"
        deps = {}
        for e in self.ENG:
            if self.cnt[e] > 0:
                deps[("c", e)] = self.cnt[e]
        for e in self.dsem:
            for i in range(N_DSEM):
                if self.dcnt[e][i] > 0:
                    deps[("d", e, i)] = self.dcnt[e][i]
        for eng in self.ENG:
            waits = self._emit_waits(eng, dict(deps), skip_self=True)
            self.n_wait += len(waits)

            def emit(e, waits=waits):
                for s, v in waits:
                    e.wait_ge(s, v)

            if waits:
                self.q[eng].append(emit)
        self.state = {}

    def run(self):
        nc = self.nc
        q = self.q
        self.q = {e: [] for e in self.ENG}
        with nc.Block() as block:
            @block.sync
            def _(e):
                for f in q["sp"]:
                    f(e)

            @block.tensor
            def _(e):
                for f in q["pe"]:
                    f(e)

            @block.scalar
            def _(e):
                for f in q["act"]:
                    f(e)

            @block.vector
            def _(e):
                for f in q["dve"]:
                    f(e)

            @block.gpsimd
            def _(e):
                for f in q["pool"]:
                    f(e)


class _StopBuild(Exception):
    pass


def build_program(L=DEPTH, NB=2, dbg=False, stop=None):
    nc = bass.Bass("TRN2", target_bir_lowering=False)

    def din(name, shape, dt=F32):
        return nc.dram_tensor(name, list(shape), dt, kind="ExternalInput").ap()

    h0T = din("h0T", [NB, 8, 128, T])
    cT_d = din("cT", [128, 8, 3])
    w_ada = din("w_ada", [L, D, 6 * D])
    b_adaT = din("b_adaT", [L, 128, 48])
    g_mixT = din("g_mixT", [L, 128, 8])
    g_ffnT = din("g_ffnT", [L, 128, 8])
    g_finT = din("g_finT", [128, 8])
    w_in_f = din("w_in", [L, D, PROJW])
    b_gateT = din("b_gateT", [L, 128, 24])
    w_abc_f = [din("w_a", [L, 512, D]), din("w_b", [L, 512, D]), din("w_c", [L, 512, D])]
    w_out_f = din("w_out", [L, D, D])
    w_up_f = din("w_up", [L, D, 2 * DFF])
    w_down_f = din("w_down", [L, DFF, D])
    w_in = nc.dram_tensor("w_in_b", [L, D, PROJW], BF16).ap()
    w_abc = [nc.dram_tensor("w_%s_b" % c_, [L, 512, D], BF16).ap() for c_ in "abc"]
    w_out = nc.dram_tensor("w_out_b", [L, D, D], BF16).ap()
    w_up = nc.dram_tensor("w_up_b", [L, D, 2 * DFF], BF16).ap()
    w_down = nc.dram_tensor("w_down_b", [L, DFF, D], BF16).ap()
    lamv = din("lamv", [L, 1, 256])
    sublnT = din("sublnT", [L, 128, 1])
    conv_wT = din("conv_wT", [L, 128, 3, 2 * NFF])
    conv_bT = din("conv_bT", [L, 128, 2 * NFF])
    rpbG = din("rpbG", [L, 8, 128, GE * 64])
    gmask = din("gmask", [128, GE * 64], BF16)
    ropeC_d = din("ropeC", [128, NLAT], BF16)
    ropeS_d = din("ropeS", [128, NLAT], BF16)
    Rm_d = din("Rm", [128, 128], BF16)
    CS_d = din("CSm", [128, 256], BF16)
    d256_d = din("dft256", [128, 2, 512], BF16)
    dftN = din("dftN", [16, 128, 2, NLAT], BF16)
    kaug_d = din("kaugc", [32, T], BF16)
    qaug_d = din("qaugc", [32, T], BF16)

    outT = nc.dram_tensor("outT", [NB, 8, 128, NLAT], F32, kind="ExternalOutput").ap()
    hT = nc.dram_tensor("hT_scr", [NB, 8, 128, T], F32).ap()
    Gd = nc.dram_tensor("G_scr", [L, 8, 128, GE * 64], BF16).ap()
    dbg_out = {}
    if dbg:
        for nm, shp, dt in (("d_aT", [128, 8, T], BF16), ("d_fa", [128, 4, T], BF16), ("d_db", [128, 4, T], BF16),
                            ("d_nc", [128, 4, T], BF16), ("d_mg", [128, 8, T], BF16), ("d_mod", [128, 48, 3], F32),
                            ("d_fT", [128, 8, T], BF16), ("d_h1", [8, 128, T], F32)):
            dbg_out[nm] = nc.dram_tensor(nm, shp, dt, kind="ExternalOutput").ap()

    with contextlib.ExitStack() as st:
        S = Sched(nc, st)

        uniq = [0]

        def sbt(stack, name, shape, dt):
            uniq[0] += 1
            return stack.enter_context(nc.sbuf_tensor("%s_%d" % (name, uniq[0]), list(shape), dt))

        def MM(out, lhsT, rhs, start, stop, r, w):
            S.op("pe", lambda e: e.matmul(out, lhsT=lhsT, rhs=rhs, start=start, stop=stop), reads=r, writes=w, inc=True)

        def ACT(out, in_, func, r, w, bias=None, scale=None):
            kw = {}
            if bias is not None:
                kw["bias"] = bias
            if scale is not None:
                kw["scale"] = scale
            S.op("act", lambda e: e.activation(out=out, in_=in_, func=func, **kw), reads=r, writes=w)

        def TT(eng, out, in0, in1, op, r, w):
            S.op(eng, lambda e: e.tensor_tensor(out=out, in0=in0, in1=in1, op=op), reads=r, writes=w)

        def TS(eng, out, in0, s1, s2, op0, op1, r, w):
            if s2 is None:
                S.op(eng, lambda e: e.tensor_scalar(out=out, in0=in0, scalar1=s1, scalar2=None, op0=op0), reads=r, writes=w)
            else:
                S.op(eng, lambda e: e.tensor_scalar(out=out, in0=in0, scalar1=s1, scalar2=s2, op0=op0, op1=op1), reads=r, writes=w)

        def STT(eng, out, in0, scalar, in1, op0, op1, r, w):
            S.op(eng, lambda e: e.scalar_tensor_tensor(out=out, in0=in0, scalar=scalar, in1=in1, op0=op0, op1=op1), reads=r, writes=w)

        def CP(eng, out, in_, r, w):
            if eng == "act":
                S.op("act", lambda e: e.copy(out=out, in_=in_), reads=r, writes=w)
            else:
                S.op(eng, lambda e: e.tensor_copy(out=out, in_=in_), reads=r, writes=w)

        def RCP(out, in_, r, w):
            S.op("dve", lambda e: e.reciprocal(out=out, in_=in_), reads=r, writes=w)

        def MSET(eng, ap, val, w):
            S.op(eng, lambda e: e.memset(ap, val), reads=(), writes=w)

        def DMA(out, in_, r, w, eng="sp"):
            S.dma(lambda e: e.dma_start(out=out, in_=in_), reads=r, writes=w, eng=eng)

        evac_rr = [0]

        def EV(out, in_, r, w):
            evac_rr[0] ^= 1
            CP("act" if evac_rr[0] else "dve", out, in_, r, w)

        aT = sbt(st, "aT", [128, 8, T], BF16)
        mg = sbt(st, "mg", [128, 8 * T], BF16)
        mgv = mg[:].rearrange("p (c t) -> p c t", c=8)
        Zv = mg[:].rearrange("p (i g x) -> p i g x", i=18, g=4)
        uv = mg[:, 0:NFF * 512].rearrange("p (i t) -> p i t", i=NFF)
        br = sbt(st, "br", [128, 4, T], BF16)
        WAR = [sbt(st, "WA", [128, 12288], BF16), sbt(st, "WB", [128, 12288], BF16)]
        ones = sbt(st, "ones", [128, 128], BF16)
        ones_f = sbt(st, "ones_f", [1, 128], F32)
        ones32 = sbt(st, "ones32", [128, 128], F32)
        epsc = sbt(st, "epsc", [128, 2], F32)
        ropeC = sbt(st, "ropeC", [128, NLAT], BF16)
        ropeS = sbt(st, "ropeS", [128, NLAT], BF16)
        Rm = sbt(st, "Rm", [128, 128], BF16)
        CSm = sbt(st, "CSm", [128, 256], BF16)
        d256 = sbt(st, "d256", [128, 2, 512], BF16)
        modT = sbt(st, "modT", [128, L, 48, 3], F32)
        colA1 = sbt(st, "colA1", [128, L, 8, 3], F32)
        colA2 = sbt(st, "colA2", [128, L, 8, 3], F32)
        gfin = sbt(st, "gfin", [128, 8], F32)
        bgate = sbt(st, "bgate", [128, L, 24], F32)
        cw = sbt(st, "cw", [128, L, 3, 2 * NFF], F32)
        cb = sbt(st, "cb", [128, L, 2 * NFF], F32)
        neglam = sbt(st, "neglam", [128, L], F32)
        sgc = sbt(st, "sgc", [128, L], F32)
        PS = [st.enter_context(nc.psum_tensor("ps%d" % i, [128, 512], F32)) for i in range(8)]

        def pk(i):
            return ("ps", i)

        AK = [("aT", t) for t in range(5)]
        MK = [("mg", t) for t in range(5)]
        BK = [("br", t) for t in range(5)]

        def wview(a, k, n):
            return WAR[a][:, 0:k * n].rearrange("p (k n) -> p k n", k=k)

        def wsub(a, off, k, n):
            return WAR[a][:, off:off + k * n].rearrange("p (k n) -> p k n", k=k)

        nphase = [0]

        def phase_end():
            S.barrier()
            S.run()
            nphase[0] += 1

        def stopped():
            return stop is not None and nphase[0] >= stop

        def prologue():
          with contextlib.ExitStack() as ph:
              cT = sbt(ph, "cT", [128, 8, 3], F32)
              scT = sbt(ph, "scT", [128, 8, 3], BF16)
              sgm = sbt(ph, "sgm", [128, 8, 3], F32)
              gmx = sbt(ph, "gmx", [128, L, 8], F32)
              gff = sbt(ph, "gff", [128, L, 8], F32)
              bada = sbt(ph, "bada", [128, L, 48], F32)
              lamr = sbt(ph, "lamr", [1, L, 4, 64], F32)
              lamp = sbt(ph, "lamp", [1, L, 2, 64], F32)
              lams = sbt(ph, "lams", [1, L, 2], F32)
              lam1 = sbt(ph, "lam1", [1, L], F32)
              subl = sbt(ph, "subl", [128, L], F32)
              gf = sbt(ph, "gf", [128, GE * 64], F32)
              gm = sbt(ph, "gm", [128, GE * 64], BF16)
              gb = [sbt(ph, "gb0", [128, GE * 64], BF16), sbt(ph, "gb1", [128, GE * 64], BF16)]

              MSET("dve", ones[:], 1.0, ["ones"])
              MSET("dve", ones_f[:], 1.0, ["ones_f"])
              MSET("dve", ones32[:], 1.0, ["ones32"])
              MSET("dve", epsc[:, 0:1], NORM_EPS, ["epsc"])
              MSET("dve", epsc[:, 1:2], SUBLN_EPS, ["epsc"])
              DMA(ropeC[:], ropeC_d, [], ["ropeC"])
              DMA(ropeS[:], ropeS_d, [], ["ropeS"])
              DMA(Rm[:], Rm_d, [], ["Rm"])
              DMA(CSm[:], CS_d, [], ["CSm"])
              DMA(d256[:], d256_d, [], ["d256"])
              DMA(cT[:], cT_d, [], ["cT"])
              DMA(gfin[:], g_finT, [], ["gfin"])
              DMA(gm[:], gmask, [], ["gm"])
              for l in range(L):
                  DMA(gmx[:, l, :], g_mixT[l], [], ["gmx"])
                  DMA(gff[:, l, :], g_ffnT[l], [], ["gff"])
                  DMA(bada[:, l, :], b_adaT[l], [], ["bada"])
                  DMA(bgate[:, l, :], b_gateT[l], [], ["bgate"])
                  DMA(cw[:, l, :, :], conv_wT[l], [], ["cw"])
                  DMA(cb[:, l, :], conv_bT[l], [], ["cb"])
                  DMA(lamr[:, l, :, :], lamv[l].rearrange("o (a d) -> o a d", a=4), [], ["lamr"])
                  DMA(subl[:, l:l + 1], sublnT[l], [], ["subl"])
              for l in range(L):
                  for (dst_, src_, rows_) in ([(w_in, w_in_f, D), (w_out, w_out_f, D), (w_up, w_up_f, D), (w_down, w_down_f, DFF)]
                                              + [(w_abc[j_], w_abc_f[j_], 512) for j_ in range(3)]):
                      for r0_ in range(0, rows_, 256):
                          DMA(dst_[l, r0_:r0_ + 256, :], src_[l, r0_:r0_ + 256, :], [], [("wbf", l)], eng="pool")
              ACT(sgm[:], cT[:], AF.Sigmoid, ["cT"], ["sgm"])
              TT("dve", scT[:], cT[:], sgm[:], ALU.mult, ["cT", "sgm"], ["scT"])
              for l in range(L):
                  TT("dve", lamp[:, l, :, :], lamr[:, l, 0:4:2, :], lamr[:, l, 1:4:2, :], ALU.mult, ["lamr"], ["lamp"])
                  S.op("dve", lambda e, l=l: e.reduce_sum(out=lams[:, l, :], in_=lamp[:, l, :, :], axis=AX.X), reads=["lamp"], writes=["lams"])
              ACT(lams[:], lams[:], AF.Exp, ["lams"], ["lams"])
              for l in range(L):
                  lam_init = 0.8 - 0.6 * math.exp(-0.3 * l)
                  TT("dve", lam1[:, l:l + 1], lams[:, l, 1:2], lams[:, l, 0:1], ALU.subtract, ["lams"], ["lam1"])
                  TS("dve", lam1[:, l:l + 1], lam1[:, l:l + 1], -lam_init, None, ALU.add, None, ["lam1"], ["lam1"])
                  TS("dve", sgc[:, l:l + 1], subl[:, l:l + 1], 1.0 - lam_init, None, ALU.mult, None, ["subl"], ["sgc"])
              MM(PS[7][:, 0:L], ones_f[:], lam1[:], True, True, ["ones_f", "lam1"], [pk(7)])
              CP("dve", neglam[:], PS[7][:, 0:L], [pk(7)], ["neglam"])
              for l in range(L):
                  for nb in range(8):
                      a = nb % 2
                      wv = wview(a, 8, 768)
                      DMA(wv, w_ada[l, :, nb * 768:(nb + 1) * 768].rearrange("(k p) n -> p k n", p=128), [], [("W", a)], eng="pool")
                      for nn in range(6):
                          j = nb * 6 + nn
                          for k in range(8):
                              MM(PS[l % 2][:, j * 3:(j + 1) * 3], wv[:, k, nn * 128:(nn + 1) * 128], scT[:, k, :], k == 0, k == 7,
                                 [("W", a), "scT"], [pk(l % 2)])
                  for r in range(3):
                      TT("dve", modT[:, l, :, r], PS[l % 2][:, r:144:3], bada[:, l, :], ALU.add, [pk(l % 2), "bada"], ["modT"])
                  for r in range(3):
                      TS("dve", colA1[:, l, :, r], modT[:, l, 8:16, r], 1.0, None, ALU.add, None, ["modT"], ["colA1"])
                      TT("dve", colA1[:, l, :, r], colA1[:, l, :, r], gmx[:, l, :], ALU.mult, ["colA1", "gmx"], ["colA1"])
                      TS("dve", colA2[:, l, :, r], modT[:, l, 32:40, r], 1.0, None, ALU.add, None, ["modT"], ["colA2"])
                      TT("dve", colA2[:, l, :, r], colA2[:, l, :, r], gff[:, l, :], ALU.mult, ["colA2", "gff"], ["colA2"])
              if dbg:
                  DMA(dbg_out["d_mod"], modT[:, 0, :, :], ["modT"], ["d_mod"])
              for l in range(L):
                  for h in range(8):
                      s_ = (l * 8 + h) % 2
                      DMA(gf[:], rpbG[l, h], [], ["gf"])
                      ACT(gf[:], gf[:], AF.Exp, ["gf"], ["gf"])
                      TT("dve", gb[s_][:], gf[:], gm[:], ALU.mult, ["gf", "gm"], [("gb", s_)])
                      DMA(Gd[l, h], gb[s_][:], [("gb", s_)], [("Gd", l, h)])
              phase_end()

        def norm_tile(hs, hkey, n, Acol, Bcol, dst, dkeys, tmp, eps_i, Dn=D, nch=8):
            sq, tf, rs = tmp
            for c in range(nch):
                ACT(sq[c % 2][:, 0:n], hs[:, c, :], AF.Square, [hkey], [("n_sq", c % 2)])
                MM(PS[7][:, 0:n], ones[:], sq[c % 2][:, 0:n], c == 0, c == nch - 1, [("n_sq", c % 2), "ones"], [pk(7)])
            ACT(rs[:, 0:n], PS[7][:, 0:n], AF.Sqrt, [pk(7), "epsc"], ["n_rs"], bias=epsc[:, eps_i:eps_i + 1], scale=1.0 / Dn)
            RCP(rs[:, 0:n], rs[:, 0:n], ["n_rs"], ["n_rs"])
            for c in range(nch):
                TT("dve", tf[c % 2][:, 0:n], hs[:, c, :], rs[:, 0:n], ALU.mult, [hkey, "n_rs"], [("n_tf", c % 2)])
                if Bcol is None:
                    ACT(dst(c), tf[c % 2][:, 0:n], AF.Identity, [("n_tf", c % 2)], dkeys, scale=Acol(c))
                else:
                    ACT(dst(c), tf[c % 2][:, 0:n], AF.Identity, [("n_tf", c % 2)], dkeys, scale=Acol(c), bias=Bcol(c))

        def norm_tmp(ph):
            return ([sbt(ph, "n_sq0", [128, 512], BF16), sbt(ph, "n_sq1", [128, 512], BF16)],
                    [sbt(ph, "n_tf0", [128, 512], F32), sbt(ph, "n_tf1", [128, 512], F32)],
                    sbt(ph, "n_rs", [128, 512], F32))

        def rr_of(b, ti):
            return 2 if ti == 4 else b

        def tiles_of(l):
            return list(range(5)) if l < L - 1 else list(range(4))

        def load_cols(a, off, src, k, n, eng="pool"):
            DMA(wsub(a, off, k, n), src.rearrange("(k p) n -> p k n", p=128), [], [("W", a)], eng=eng)

        def pre_fa(l):
            load_cols(0, 0, w_in[l, :, 0:512], 8, 512)

        def pre_merge(l, j):
            load_cols(1, 0, w_in[l, :, 3584 + j * 1024:3584 + (j + 1) * 1024], 8, 1024)
            load_cols(1, 8192, w_abc[j][l], 4, 1024)

        def pre_qkv(l, base):
            for i in range(3):
                load_cols(0, i * 4096, w_in[l, :, base + i * 512:base + (i + 1) * 512], 8, 512)

        def pre_out(l):
            load_cols(0, 0, w_out[l], 8, 1024)

        def ffn_groups():
            return [(i0, min(4, NFF - i0)) for i0 in range(0, NFF, 4)]

        def pre_up(l, a, i0, gi):
            load_cols(a, 0, w_up[l, :, i0 * 128:(i0 + gi) * 128], 8, gi * 128)
            load_cols(a, 4096, w_up[l, :, DFF + i0 * 128:DFF + (i0 + gi) * 128], 8, gi * 128)

        def phase_n1_first(b, pre):
            if stopped():
                return
            with contextlib.ExitStack() as ph:
                ht = [sbt(ph, "ht0", [128, 8, 512], F32), sbt(ph, "ht1", [128, 8, 512], F32)]
                tmp = norm_tmp(ph)
                pre()
                for ti, (t0, n) in enumerate(TILES):
                    s_ = ti % 2
                    r = rr_of(b, ti)
                    DMA(ht[s_][:, :, 0:n], h0T[b, :, :, t0:t0 + n].rearrange("c p t -> p c t"), [], [("ht", s_)])
                    norm_tile(ht[s_][:, :, 0:n], ("ht", s_), n,
                              lambda c, r=r: colA1[:, 0, c, r:r + 1], lambda c, r=r: modT[:, 0, c, r:r + 1],
                              lambda c, t0=t0, n=n: aT[:, c, t0:t0 + n], [AK[ti]], tmp, 0)
                if dbg and b == 0:
                    DMA(dbg_out["d_aT"], aT[:], AK, ["d_aT"])
                phase_end()

        def phase_fa(b, l, pre):
            if stopped():
                return
            last = l == L - 1
            with contextlib.ExitStack() as ph:
                uT = sbt(ph, "uT", [128, 4, T], BF16)
                dt_ = [sbt(ph, "dft0", [128, 2, 512], BF16), sbt(ph, "dft1", [128, 2, 512], BF16), sbt(ph, "dft2", [128, 2, 512], BF16)]
                pre()
                wfa = wview(0, 8, 512)
                bank = 0
                for ti, (t0, n) in enumerate(TILES):
                    for g in range(4):
                        p = PS[bank % 4]
                        for k in range(8):
                            MM(p[:, 0:n], wfa[:, k, g * 128:(g + 1) * 128], aT[:, k, t0:t0 + n], k == 0, k == 7,
                               [("W", 0), AK[ti]], [pk(bank % 4)])
                        EV(uT[:, g, t0:t0 + n], p[:, 0:n], [pk(bank % 4)], [("uT", ti)])
                        bank += 1
                for i in range(18):
                    ti = min(i // 4, 4)
                    for half in range(2):
                        bi = 4 + (i % 2) * 2 + half
                        for gg in range(2):
                            g = half * 2 + gg
                            MM(PS[bi][:, gg * 256:(gg + 1) * 256], uT[:, g, i * 128:(i + 1) * 128], CSm[:], True, True,
                               [("uT", ti), "CSm"], [pk(bi)])
                        EV(Zv[:, i, half * 2:half * 2 + 2, :], PS[bi][:].rearrange("p (g x) -> p g x", g=2), [pk(bi)], [("Z", i)])
                sc_lat = 1.0 / math.sqrt(NLAT * 128.0)
                cnt = 0
                for j in range(4):
                    for i in range(16):
                        s_ = cnt % 3
                        cnt += 1
                        DMA(dt_[s_][:], dftN[i, :, :, j * 512:(j + 1) * 512], [], [("dft", s_)])
                        for g in range(4):
                            bi = (j % 2) * 4 + g
                            MM(PS[bi][:], Zv[:, i, g, 0:128], dt_[s_][:, 0, :], i == 0, False, [("Z", i), ("dft", s_)], [pk(bi)])
                            MM(PS[bi][:], Zv[:, i, g, 128:256], dt_[s_][:, 1, :], False, i == 15, [("Z", i), ("dft", s_)], [pk(bi)])
                    for g in range(4):
                        bi = (j % 2) * 4 + g
                        if g % 2 == 0:
                            ACT(br[:, g, j * 512:(j + 1) * 512], PS[bi][:], AF.Identity, [pk(bi)], [BK[j]], scale=sc_lat)
                        else:
                            TS("dve", br[:, g, j * 512:(j + 1) * 512], PS[bi][:], sc_lat, None, ALU.mult, None, [pk(bi)], [BK[j]])
                if not last:
                    sc_c = 1.0 / math.sqrt(NCTX * 128.0)
                    for g in range(4):
                        bi = g
                        for i in range(2):
                            MM(PS[bi][:, 0:256], Zv[:, 16 + i, g, 0:128], d256[:, i, 0:256], i == 0, False, [("Z", 16 + i), "d256"], [pk(bi)])
                            MM(PS[bi][:, 0:256], Zv[:, 16 + i, g, 128:256], d256[:, i, 256:512], False, i == 1, [("Z", 16 + i), "d256"], [pk(bi)])
                        TS("dve", br[:, g, NLAT:T], PS[bi][:, 0:256], sc_c, None, ALU.mult, None, [pk(bi)], [BK[4]])
                if dbg and b == 0 and l == 0:
                    DMA(dbg_out["d_fa"], br[:], BK, ["d_fa"])
                phase_end()

        def phase_merge(b, l, j, pre):
            if stopped():
                return
            with contextlib.ExitStack() as ph:
                gate = [sbt(ph, "gate0", [128, 512], F32), sbt(ph, "gate1", [128, 512], F32)]
                tmpm = [sbt(ph, "tmpm0", [128, 512], F32), sbt(ph, "tmpm1", [128, 512], F32)]
                pre()
                wg = wsub(1, 0, 8, 1024)
                wx = wsub(1, 8192, 4, 1024)
                cnt = 0
                for ti in tiles_of(l):
                    t0, n = TILES[ti]
                    for n8 in range(8):
                        s_ = cnt % 2
                        b0 = (cnt % 2) * 2
                        cnt += 1
                        for k in range(8):
                            MM(PS[b0][:, 0:n], wg[:, k, n8 * 128:(n8 + 1) * 128], aT[:, k, t0:t0 + n], k == 0, k == 7,
                               [("W", 1), AK[ti]], [pk(b0)])
                        for g in range(4):
                            MM(PS[b0 + 1][:, 0:n], wx[:, g, n8 * 128:(n8 + 1) * 128], br[:, g, t0:t0 + n], g == 0, g == 3,
                               [("W", 1), BK[ti]], [pk(b0 + 1)])
                        ACT(gate[s_][:, 0:n], PS[b0][:, 0:n], AF.Sigmoid, [pk(b0), "bgate"], [("gate", s_)],
                            bias=bgate[:, l, j * 8 + n8:j * 8 + n8 + 1], scale=1.0)
                        if j == 0:
                            TT("dve", mgv[:, n8, t0:t0 + n], gate[s_][:, 0:n], PS[b0 + 1][:, 0:n], ALU.mult,
                               [("gate", s_), pk(b0 + 1)], [MK[ti]])
                        else:
                            TT("dve", tmpm[s_][:, 0:n], gate[s_][:, 0:n], PS[b0 + 1][:, 0:n], ALU.mult,
                               [("gate", s_), pk(b0 + 1)], [("tmpm", s_)])
                            TT("pool", mgv[:, n8, t0:t0 + n], mgv[:, n8, t0:t0 + n], tmpm[s_][:, 0:n], ALU.add,
                               [("tmpm", s_), MK[ti]], [MK[ti]])
                if dbg and b == 0 and l == 0 and j == 2:
                    DMA(dbg_out["d_mg"], mgv, MK, ["d_mg"])
                phase_end()

        def attn_pipeline(nk, s_fn, e_fn, v_fn):
            for step in range(nk + 2):
                if step < nk:
                    s_fn(step)
                if 1 <= step < nk + 1:
                    e_fn(step - 1)
                if step >= 2:
                    v_fn(step - 2)

        def phase_da(b, l, pre):
            if stopped():
                return
            last = l == L - 1
            with contextlib.ExitStack() as ph:
                qT = sbt(ph, "qT", [128, T], BF16)
                kT = sbt(ph, "kT", [128, T], BF16)
                Vt = sbt(ph, "Vt", [128, 18, 128], BF16)
                xb = [sbt(ph, "xb0", [128, 512], BF16), sbt(ph, "xb1", [128, 512], BF16)]
                t1 = [sbt(ph, "t1_0", [128, 512], F32), sbt(ph, "t1_1", [128, 512], F32)]
                t2 = [sbt(ph, "t2_0", [128, 512], F32), sbt(ph, "t2_1", [128, 512], F32)]
                Pb = [sbt(ph, "P%d" % i, [128, 512], BF16) for i in range(3)]
                r0 = sbt(ph, "r0", [128, 512], F32)
                r1 = sbt(ph, "r1", [128, 512], F32)
                o0 = sbt(ph, "o0", [128, 512], F32)
                o1 = sbt(ph, "o1", [128, 512], F32)
                sqo = sbt(ph, "sqo", [128, 512], BF16)
                rso = sbt(ph, "rso", [128, 512], F32)
                Pac = [[sbt(ph, "Pac%d%d" % (m_, e_), [128, 512], F32) for e_ in range(2)] for m_ in range(2)]
                pre()
                wq = wsub(0, 0, 8, 512)
                wk = wsub(0, 4096, 8, 512)
                wv = wsub(0, 8192, 8, 512)
                qtiles = tiles_of(l)
                rc = [0]

                def proj_rope(w, dstT, dkey, h, tlist):
                    pend_ = []
                    for ti in tlist:
                        t0, n = TILES[ti]
                        s_ = rc[0] % 2
                        rc[0] += 1
                        pb_ = s_ * 2
                        for k in range(8):
                            MM(PS[pb_][:, 0:n], w[:, k, h * 128:(h + 1) * 128], aT[:, k, t0:t0 + n], k == 0, k == 7,
                               [("W", 0), AK[ti]], [pk(pb_)])
                        if ti < 4:
                            CP("act", xb[s_][:, 0:n], PS[pb_][:, 0:n], [pk(pb_)], [("xb", s_)])

                            def rope(s_=s_, pb_=pb_, t0=t0, n=n, ti=ti):
                                MM(PS[pb_ + 1][:, 0:n], Rm[:], xb[s_][:, 0:n], True, True, ["Rm", ("xb", s_)], [pk(pb_ + 1)])
                                TT("dve", t1[s_][:, 0:n], xb[s_][:, 0:n], ropeC[:, t0:t0 + n], ALU.mult, [("xb", s_), "ropeC"], [("t1", s_)])
                                TT("dve", t2[s_][:, 0:n], PS[pb_ + 1][:, 0:n], ropeS[:, t0:t0 + n], ALU.mult, [pk(pb_ + 1), "ropeS"], [("t2", s_)])
                                TT("pool", dstT[:, t0:t0 + n], t1[s_][:, 0:n], t2[s_][:, 0:n], ALU.add, [("t1", s_), ("t2", s_)], [(dkey, ti)])

                            if pend_:
                                pend_.pop()()
                            pend_.append(rope)
                        else:
                            CP("act", dstT[:, t0:t0 + n], PS[pb_][:, 0:n], [pk(pb_)], [(dkey, ti)])
                    if pend_:
                        pend_.pop()()

                for h in range(4):
                    proj_rope(wk, kT, "kT", h, list(range(5)))
                    for i4 in range(5):
                        ni = 4 if i4 < 4 else 2
                        bi = 4 + (i4 % 2)
                        for ii in range(ni):
                            i = i4 * 4 + ii
                            ti = min(i // 4, 4)
                            for k in range(8):
                                MM(PS[bi][:, ii * 128:(ii + 1) * 128], aT[:, k, i * 128:(i + 1) * 128], wv[:, k, h * 128:(h + 1) * 128],
                                   k == 0, k == 7, [("W", 0), AK[ti]], [pk(bi)])
                        EV(Vt[:, i4 * 4:i4 * 4 + ni, :], PS[bi][:, 0:ni * 128].rearrange("p (i e) -> p i e", i=ni), [pk(bi)], [("Vt", i4)])
                    proj_rope(wq, qT, "qT", h, qtiles)
                    steps = []
                    for ti in qtiles:
                        keys = list(range(18)) if ti < 4 else [16, 17]
                        for m in range(2):
                            for sidx, i in enumerate(keys):
                                steps.append((ti, m, sidx, len(keys), i))
                    NS = len(steps)

                    def s_fn(g):
                        ti, m, sidx, nk, i = steps[g]
                        t0, n = TILES[ti]
                        kti = min(i // 4, 4)
                        MM(PS[g % 3][:, 0:n], kT[m * 64:(m + 1) * 64, i * 128:(i + 1) * 128], qT[m * 64:(m + 1) * 64, t0:t0 + n],
                           True, True, [("kT", kti), ("qT", ti)], [pk(g % 3)])

                    def e_fn(g):
                        ti, m, sidx, nk, i = steps[g]
                        n = TILES[ti][1]
                        ACT(Pb[g % 3][:, 0:n], PS[g % 3][:, 0:n], AF.Exp, [pk(g % 3)], [("P", g % 3)], scale=0.125)

                    def v_fn(g):
                        ti, m, sidx, nk, i = steps[g]
                        n = TILES[ti][1]
                        pa = PS[3 + m]
                        pd = PS[5 + m]
                        MM(pa[:, 0:n], Vt[:, i, :], Pb[g % 3][:, 0:n], sidx == 0, sidx == nk - 1,
                           [("Vt", i // 4), ("P", g % 3)], [pk(3 + m)])
                        e_ = sidx % 2
                        eng_ = "pool" if e_ == 0 else "dve"
                        if sidx < 2:
                            CP(eng_, Pac[m][e_][:, 0:n], Pb[g % 3][:, 0:n], [("P", g % 3)], [("Pac", m, e_)])
                        else:
                            TT(eng_, Pac[m][e_][:, 0:n], Pac[m][e_][:, 0:n], Pb[g % 3][:, 0:n], ALU.add,
                               [("P", g % 3), ("Pac", m, e_)], [("Pac", m, e_)])
                        if sidx == nk - 1:
                            MM(pd[:, 0:n], ones32[:], Pac[m][0][:, 0:n], True, False, ["ones32", ("Pac", m, 0)], [pk(5 + m)])
                            MM(pd[:, 0:n], ones32[:], Pac[m][1][:, 0:n], False, True, ["ones32", ("Pac", m, 1)], [pk(5 + m)])

                    def fin1(ti, h=h):
                        t0, n = TILES[ti]
                        RCP(r0[:, 0:n], PS[5][:, 0:n], [pk(5)], ["r0"])
                        RCP(r1[:, 0:n], PS[6][:, 0:n], [pk(6)], ["r1"])
                        TT("dve", o0[:, 0:n], PS[3][:, 0:n], r0[:, 0:n], ALU.mult, [pk(3), "r0"], ["o0"])
                        TT("dve", o1[:, 0:n], PS[4][:, 0:n], r1[:, 0:n], ALU.mult, [pk(4), "r1"], ["o1"])
                        STT("dve", o0[:, 0:n], o1[:, 0:n], neglam[:, l:l + 1], o0[:, 0:n], ALU.mult, ALU.add, ["o0", "o1", "neglam"], ["o0"])
                        ACT(sqo[:, 0:n], o0[:, 0:n], AF.Square, ["o0"], ["sqo"])

                    def fin2(ti, h=h):
                        t0, n = TILES[ti]
                        MM(PS[7][:, 0:n], ones[:], sqo[:, 0:n], True, True, ["ones", "sqo"], [pk(7)])
                        ACT(rso[:, 0:n], PS[7][:, 0:n], AF.Ln, [pk(7), "epsc"], ["rso"], bias=epsc[:, 1:2], scale=1.0 / 128.0)
                        ACT(rso[:, 0:n], rso[:, 0:n], AF.Exp, ["rso"], ["rso"], scale=-0.5)
                        TT("dve", o0[:, 0:n], o0[:, 0:n], rso[:, 0:n], ALU.mult, ["o0", "rso"], ["o0"])
                        ACT(br[:, h, t0:t0 + n], o0[:, 0:n], AF.Identity, ["o0", "sgc"], [BK[ti]], scale=sgc[:, l:l + 1])

                    later = []
                    for step in range(NS + 2):
                        if step < NS:
                            s_fn(step)
                        if 1 <= step < NS + 1:
                            e_fn(step - 1)
                        if step >= 2:
                            g = step - 2
                            v_fn(g)
                            ti, m, sidx, nk, i = steps[g]
                            if m == 1 and sidx == nk - 1:
                                while later:
                                    fin2(later.pop(0)[1])
                                fin1(ti)
                                later.append((step + 6, ti))
                        while later and later[0][0] <= step:
                            fin2(later.pop(0)[1])
                    while later:
                        fin2(later.pop(0)[1])
                if dbg and b == 0 and l == 0:
                    DMA(dbg_out["d_db"], br[:], BK, ["d_db"])
                phase_end()

        def phase_na(b, l, pre):
            if stopped():
                return
            last = l == L - 1
            with contextlib.ExitStack() as ph:
                qa = sbt(ph, "qa", [96, T], BF16)
                ka = sbt(ph, "ka", [96, T], BF16)
                Vt = sbt(ph, "Vtn", [128, 18, 512], BF16)
                Gt = [sbt(ph, "Gt0", [128, GE * 64], BF16), sbt(ph, "Gt1", [128, GE * 64], BF16)]
                Pb = [sbt(ph, "Pn%d" % i, [128, 512], BF16) for i in range(3)]
                rn = sbt(ph, "rn", [64, 512], F32)
                Vh = [sbt(ph, "Vh%d" % i, [128, 18, 128], BF16) for i in range(2)]
                pre()
                wq = wsub(0, 0, 8, 512)
                wk = wsub(0, 4096, 8, 512)
                wv = wsub(0, 8192, 8, 512)
                qtiles = tiles_of(l)
                DMA(qa[64:96, :], qaug_d, [], ["qa_aug"])
                DMA(ka[64:96, :], kaug_d, [], ["ka_aug"])
                for i_ in range(2):
                    MSET("pool", Vh[i_][:, :, 64:128], 1.0, [("Vh", i_)])
                for i in range(18):
                    ti = min(i // 4, 4)
                    bi = 4 + i % 2
                    for k in range(8):
                        MM(PS[bi][:], aT[:, k, i * 128:(i + 1) * 128], wv[:, k, :], k == 0, k == 7, [("W", 0), AK[ti]], [pk(bi)])
                    EV(Vt[:, i, :], PS[bi][:], [pk(bi)], [("Vt", i)])
                rc = 0
                for h in range(8):
                    gs = h % 2
                    DMA(Gt[gs][:], Gd[l, h], [("Gd", l, h)], [("Gt", gs)])
                    CP("pool", Vh[gs][:, :, 0:64], Vt[:, :, h * 64:(h + 1) * 64], [("Vt", i_) for i_ in range(18)], [("Vh", gs)])
                    for (w, dst, dkey, tl) in ((wk, ka, "ka", list(range(5))), (wq, qa, "qa", qtiles)):
                        for ti in tl:
                            t0, n = TILES[ti]
                            pb_ = 6 + rc % 2
                            rc += 1
                            for k in range(8):
                                MM(PS[pb_][0:64, 0:n], w[:, k, h * 64:(h + 1) * 64], aT[:, k, t0:t0 + n], k == 0, k == 7,
                                   [("W", 0), AK[ti]], [pk(pb_)])
                            EV(dst[0:64, t0:t0 + n], PS[pb_][0:64, 0:n], [pk(pb_)], [(dkey, ti)])
                    hp = (h % 2) * 64
                    for ti in qtiles:
                        t0, n = TILES[ti]
                        if ti < 4:
                            rb = 8 * ti
                            kr_start, nlat = (0, 6) if ti == 0 else ((20, 6) if ti == 3 else (rb - 4, 8))
                            keys = [(kr_start // 2 + j_, 14 - (kr_start + 2 * j_) + rb) for j_ in range(nlat)] + [(16, None), (17, None)]
                        else:
                            keys = [(16, None), (17, None)]
                        nk = len(keys)

                        def s_fn(sidx, keys=keys, t0=t0, n=n, ti=ti):
                            i = keys[sidx][0]
                            kti = min(i // 4, 4)
                            MM(PS[sidx % 3][:, 0:n], ka[0:96, i * 128:(i + 1) * 128], qa[0:96, t0:t0 + n], True, True,
                               [("ka", kti), "ka_aug", ("qa", ti), "qa_aug"], [pk(sidx % 3)])

                        def e_fn(sidx, keys=keys, n=n, gs=gs):
                            e0 = keys[sidx][1]
                            ACT(Pb[sidx % 3][:, 0:n], PS[sidx % 3][:, 0:n], AF.Exp, [pk(sidx % 3)], [("P", sidx % 3)], scale=0.125)
                            if e0 is not None:
                                TT("dve", Pb[sidx % 3][:, 0:n], Pb[sidx % 3][:, 0:n], Gt[gs][:, e0 * 64:e0 * 64 + n], ALU.mult,
                                   [("P", sidx % 3), ("Gt", gs)], [("P", sidx % 3)])

                        def v_fn(sidx, keys=keys, n=n, nk=nk, gs=gs):
                            i = keys[sidx][0]
                            MM(PS[3][:, 0:n], Vh[gs][:, i, :], Pb[sidx % 3][:, 0:n], sidx == 0, sidx == nk - 1,
                               [("Vh", gs), ("P", sidx % 3)], [pk(3)])

                        attn_pipeline(nk, s_fn, e_fn, v_fn)
                        RCP(rn[:, 0:n], PS[3][64:128, 0:n], [pk(3)], ["rn"])
                        TT("dve", br[hp:hp + 64, h // 2, t0:t0 + n], PS[3][0:64, 0:n], rn[:, 0:n], ALU.mult, [pk(3), "rn"], [BK[ti]])
                if dbg and b == 0 and l == 0:
                    DMA(dbg_out["d_nc"], br[:], BK, ["d_nc"])
                phase_end()

        def phase_out(b, l, pre):
            if stopped():
                return
            with contextlib.ExitStack() as ph:
                ht = [sbt(ph, "ht0", [128, 8, 512], F32), sbt(ph, "ht1", [128, 8, 512], F32)]
                tmp = norm_tmp(ph)
                pre()
                wo = wsub(0, 0, 8, 1024)
                hsrc = h0T if l == 0 else hT
                cnt = 0
                for ti in tiles_of(l):
                    t0, n = TILES[ti]
                    s_ = ti % 2
                    r = rr_of(b, ti)
                    rk = [] if l == 0 else [("hT", ti)]
                    DMA(ht[s_][:, :, 0:n], hsrc[b, :, :, t0:t0 + n].rearrange("c p t -> p c t"), rk, [("ht", s_)])
                    for n8 in range(8):
                        bi = cnt % 4
                        cnt += 1
                        for k in range(8):
                            MM(PS[bi][:, 0:n], wo[:, k, n8 * 128:(n8 + 1) * 128], mgv[:, k, t0:t0 + n], k == 0, k == 7,
                               [("W", 0), MK[ti]], [pk(bi)])
                        STT("dve", ht[s_][:, n8, 0:n], PS[bi][:, 0:n], modT[:, l, 16 + n8, r:r + 1], ht[s_][:, n8, 0:n], ALU.mult, ALU.add,
                            [pk(bi), ("ht", s_), "modT"], [("ht", s_)])
                    DMA(hT[b, :, :, t0:t0 + n].rearrange("c p t -> p c t"), ht[s_][:, :, 0:n], [("ht", s_)], [("hT", ti)])
                    norm_tile(ht[s_][:, :, 0:n], ("ht", s_), n,
                              lambda c, r=r: colA2[:, l, c, r:r + 1], lambda c, r=r: modT[:, l, 24 + c, r:r + 1],
                              lambda c, t0=t0, n=n: aT[:, c, t0:t0 + n], [AK[ti]], tmp, 0)
                if dbg and b == 0 and l == 0:
                    DMA(dbg_out["d_fT"], aT[:], AK, ["d_fT"])
                phase_end()

        def phase_ffn(b, l, pre_next):
            if stopped():
                return
            last = l == L - 1
            groups = ffn_groups()
            with contextlib.ExitStack() as ph:
                ht = sbt(ph, "htf", [128, 8, 512], F32)
                tmp = norm_tmp(ph)
                hb = sbt(ph, "hb", [128, 8, 8], BF16)
                hpre = sbt(ph, "hpre", [128, 2 * NFF, 6], F32)
                pa_ = [sbt(ph, "pa0", [128, 514], F32), sbt(ph, "pa1", [128, 514], F32)]
                pb_ = [sbt(ph, "pb0", [128, 514], F32), sbt(ph, "pb1", [128, 514], F32)]
                ya = [sbt(ph, "ya0", [128, 512], F32), sbt(ph, "ya1", [128, 512], F32)]
                yb = [sbt(ph, "yb0", [128, 512], F32), sbt(ph, "yb1", [128, 512], F32)]
                wdn = [sbt(ph, "wdn%d" % i, [128, 4, 512], BF16) for i in range(2)]
                tl = tiles_of(l)
                for c_, col in enumerate((511, 512, 1023, 1024, 1535, 1536)):
                    CP("dve", hb[:, :, c_:c_ + 1], aT[:, :, col:col + 1], [AK[col // 512]], ["hb"])
                gcount = 0
                dcount = 0
                icount = 0
                for tidx, ti in enumerate(tl):
                    t0, n = TILES[ti]
                    r = rr_of(b, ti)
                    hl = (t0 // 512) * 2 - 2 if t0 not in (0, NLAT) else None
                    hr = (t0 // 512) * 2 + 1 if (t0 + n) not in (NLAT, T) else None
                    DMA(ht[:, :, 0:n], hT[b, :, :, t0:t0 + n].rearrange("c p t -> p c t"), [("hT", ti)], ["htf"])
                    pend = []
                    for gidx, (i0, gi) in enumerate(groups):
                        a = (gcount + 1) % 2
                        gcount += 1
                        if gidx + 1 < len(groups):
                            pre_up(l, (gcount + 1) % 2, *groups[gidx + 1])
                        elif tidx + 1 < len(tl):
                            pre_up(l, (gcount + 1) % 2, *groups[0])
                        wa = wsub(a, 0, 8, gi * 128)
                        wb = wsub(a, 4096, 8, gi * 128)
                        for ii in range(gi):
                            i = i0 + ii
                            s_ = icount % 2
                            icount += 1
                            for (wsel, bi, pre_t, pkey) in ((wa, 4, pa_[s_], ("pa", s_)), (wb, 5, pb_[s_], ("pb", s_))):
                                for k in range(8):
                                    MM(PS[bi][:, 0:n], wsel[:, k, ii * 128:(ii + 1) * 128], aT[:, k, t0:t0 + n], k == 0, k == 7,
                                       [("W", a), AK[ti]], [pk(bi)])
                                CP("act", pre_t[:, 1:n + 1], PS[bi][:, 0:n], [pk(bi)], [pkey])
                            if tidx == 0:
                                for (wsel, j_, co) in ((wa, i, s_ * 12), (wb, NFF + i, s_ * 12 + 6)):
                                    for k in range(8):
                                        MM(PS[6][:, co:co + 6], wsel[:, k, ii * 128:(ii + 1) * 128], hb[:, k, 0:6], k == 0, k == 7,
                                           [("W", a), "hb"], [pk(6)])
                                for (j_, co) in ((i, s_ * 12), (NFF + i, s_ * 12 + 6)):
                                    CP("dve", hpre[:, j_, :], PS[6][:, co:co + 6], [pk(6)], [("hpre", j_)])
                            for (pre_t, pkey, j_) in ((pa_[s_], ("pa", s_), i), (pb_[s_], ("pb", s_), NFF + i)):
                                for (hx, dcol) in ((hl, 0), (hr, n + 1)):
                                    if hx is None:
                                        MSET("pool", pre_t[:, dcol:dcol + 1], 0.0, [pkey])
                                    else:
                                        CP("pool", pre_t[:, dcol:dcol + 1], hpre[:, j_, hx:hx + 1], [("hpre", j_)], [pkey])
                            def chain(s_=s_, i=i, n=n):
                                for (pre_t, pkey, y_, ykey, j_) in ((pa_[s_], ("pa", s_), ya[s_], ("ya", s_), i), (pb_[s_], ("pb", s_), yb[s_], ("yb", s_), NFF + i)):
                                    ACT(y_[:, 0:n], pre_t[:, 1:n + 1], AF.Identity, [pkey, "cw", "cb"], [ykey], scale=cw[:, l, 1, j_:j_ + 1], bias=cb[:, l, j_:j_ + 1])
                                    STT("dve", y_[:, 0:n], pre_t[:, 0:n], cw[:, l, 0, j_:j_ + 1], y_[:, 0:n], ALU.mult, ALU.add, [pkey, ykey, "cw"], [ykey])
                                    STT("dve", y_[:, 0:n], pre_t[:, 2:n + 2], cw[:, l, 2, j_:j_ + 1], y_[:, 0:n], ALU.mult, ALU.add, [pkey, ykey, "cw"], [ykey])
                                ACT(ya[s_][:, 0:n], ya[s_][:, 0:n], AF.Silu, [("ya", s_)], [("ya", s_)])
                                TT("pool", uv[:, i, 0:n], ya[s_][:, 0:n], yb[s_][:, 0:n], ALU.mult, [("ya", s_), ("yb", s_)], [("u", i)])

                            if pend:
                                pend.pop()()
                            pend.append(chain)
                    if pend:
                        pend.pop()()
                    if tidx + 1 == len(tl):
                        pre_next()
                    for half in range(2):
                        for gidx, (i0, gi) in enumerate(groups):
                            ds_ = dcount % 2
                            dcount += 1
                            DMA(wdn[ds_][:, 0:gi, :], w_down[l, i0 * 128:(i0 + gi) * 128, half * 512:(half + 1) * 512].rearrange("(i p) n -> p i n", p=128),
                                [], [("wdn", ds_)], eng="pool")
                            for ii in range(gi):
                                i = i0 + ii
                                for nn in range(4):
                                    MM(PS[nn][:, 0:n], wdn[ds_][:, ii, nn * 128:(nn + 1) * 128], uv[:, i, 0:n], i == 0, i == NFF - 1,
                                       [("wdn", ds_), ("u", i)], [pk(nn)])
                        for nn in range(4):
                            n8 = half * 4 + nn
                            STT("dve", ht[:, n8, 0:n], PS[nn][:, 0:n], modT[:, l, 40 + n8, r:r + 1], ht[:, n8, 0:n], ALU.mult, ALU.add,
                                [pk(nn), "htf", "modT"], ["htf"])
                    if not last:
                        DMA(hT[b, :, :, t0:t0 + n].rearrange("c p t -> p c t"), ht[:, :, 0:n], ["htf"], [("hT", ti)])
                        norm_tile(ht[:, :, 0:n], "htf", n,
                                  lambda c, r=r: colA1[:, l + 1, c, r:r + 1], lambda c, r=r: modT[:, l + 1, c, r:r + 1],
                                  lambda c, t0=t0, n=n: aT[:, c, t0:t0 + n], [AK[ti]], tmp, 0)
                    else:
                        norm_tile(ht[:, :, 0:n], "htf", n, lambda c: gfin[:, c:c + 1], None,
                                  lambda c, n=n: ht[:, c, 0:n], ["htf"], tmp, 0)
                        DMA(outT[b, :, :, t0:t0 + n].rearrange("c p t -> p c t"), ht[:, :, 0:n], ["htf"], [("out", b, ti)])
                if dbg and b == 0 and l == 0 and not last:
                    pass
                phase_end()

        def nop():
            pass

        try:
          prologue()
          for b in range(NB):
            phase_n1_first(b, lambda: pre_fa(0))
            for l in range(L):
                phase_fa(b, l, lambda l=l: pre_merge(l, 0))
                phase_merge(b, l, 0, lambda l=l: pre_qkv(l, 512))
                phase_da(b, l, lambda l=l: pre_merge(l, 1))
                phase_merge(b, l, 1, lambda l=l: pre_qkv(l, 2048))
                phase_na(b, l, lambda l=l: pre_merge(l, 2))
                phase_merge(b, l, 2, lambda l=l: pre_out(l))
                phase_out(b, l, lambda l=l: pre_up(l, 1, *ffn_groups()[0]))
                if l + 1 < L:
                    phase_ffn(b, l, lambda l=l: pre_fa(l + 1))
                else:
                    phase_ffn(b, l, nop)
        except _StopBuild:
            pass
        S.barrier()
        S.run()
    build_program.stats = (S.n_ins, S.n_wait)
    return nc


def _consts():
    bf = ml_dtypes.bfloat16
    c = {}
    t = np.arange(NLAT)
    pos = np.stack([(t // 64).astype(np.float32), (t % 64).astype(np.float32)], 0)
    inv = (10000.0 ** (-np.arange(16, dtype=np.float32) / 16)).astype(np.float32)
    p = np.arange(128)
    axis = (p % 64) // 32
    half = (p % 32) // 16
    f = p % 16
    ang = pos[axis, :] * inv[f][:, None]
    c["ropeC"] = np.cos(ang).astype(bf)
    c["ropeS"] = np.sin(ang).astype(bf)
    R = np.zeros((128, 128), np.float32)
    for q in range(128):
        if half[q] == 0:
            R[q + 16, q] = -1.0
        else:
            R[q - 16, q] = 1.0
    c["Rm"] = R.astype(bf)
    cc = np.arange(128)
    phi = 2 * np.pi * np.outer(cc, cc) / 128.0
    c["CSm"] = np.concatenate([np.cos(phi), np.sin(phi)], 1).astype(bf)
    n2 = np.arange(NCTX)
    th = 2 * np.pi * (np.outer(n2, n2) % NCTX) / NCTX
    d256 = np.concatenate([np.cos(th), -np.sin(th)], 1).reshape(2, 128, 512).transpose(1, 0, 2)
    c["dft256"] = np.ascontiguousarray(d256).astype(bf)
    nn = np.arange(NLAT)
    thn = 2 * np.pi * ((np.outer(nn, nn) % NLAT).astype(np.float64)) / NLAT
    dN = np.stack([np.cos(thn), -np.sin(thn)], 1)
    c["dftN"] = np.ascontiguousarray(dN.reshape(16, 128, 2, NLAT)).astype(bf)
    ka = np.zeros((32, T), np.float32)
    qa = np.zeros((32, T), np.float32)
    for tok in range(NLAT):
        r = tok // 64
        ka[r, tok] = 1.0
        r0 = min(max(r - 4, 0), 24)
        qa[:, tok] = MASKV
        qa[r0:r0 + 8, tok] = 0.0
    c["kaugc"] = ka.astype(bf)
    c["qaugc"] = qa.astype(bf)
    dr = np.zeros((128, GE, 64), np.int64)
    dc = np.zeros((128, GE, 64), np.int64)
    ok = np.zeros((128, GE, 64), bool)
    for e in range(GE):
        d = 21 - e
        for pp in range(128):
            drr = d + (1 if pp >= 64 else 0)
            kc = pp % 64
            for qc in range(64):
                c0 = min(max(qc - 8, 0), 48)
                v = (0 <= drr <= 14) and (c0 <= kc < c0 + 16)
                ok[pp, e, qc] = v
                if v:
                    dr[pp, e, qc] = drr
                    dc[pp, e, qc] = kc - qc + 15
    c["_dr"] = dr.reshape(128, GE * 64)
    c["_dc"] = dc.reshape(128, GE * 64)
    c["gmask"] = ok.reshape(128, GE * 64).astype(np.float32).astype(bf)
    return c


_CONST_CACHE = {}


def _prep_inputs(inp, L, NB, ncores):
    if "c" not in _CONST_CACHE:
        _CONST_CACHE["c"] = _consts()
    C = _CONST_CACHE["c"]
    f32 = np.float32
    shared = {}
    for k in ("w_ada", "w_in", "w_a", "w_b", "w_c", "w_out", "w_up", "w_down"):
        shared[k] = np.ascontiguousarray(np.asarray(inp[k], f32)[:L])

    def fm(a, nch):
        a = np.asarray(a, f32)[:L]
        return np.ascontiguousarray(a.reshape(L, nch, 128).transpose(0, 2, 1))

    shared["b_adaT"] = fm(inp["b_ada"], 48)
    shared["g_mixT"] = fm(inp["g_mix"], 8)
    shared["g_ffnT"] = fm(inp["g_ffn"], 8)
    shared["g_finT"] = np.ascontiguousarray(np.asarray(inp["g_final"], f32).reshape(8, 128).T)
    shared["b_gateT"] = fm(inp["b_gate"], 24)
    shared["lamv"] = np.ascontiguousarray(np.asarray(inp["lam"], f32)[:L].reshape(L, 1, 256))
    shared["sublnT"] = np.ascontiguousarray(np.asarray(inp["subln_g"], f32)[:L].reshape(L, 128, 1))
    cwt = np.asarray(inp["conv_w"], f32)[:L].reshape(L, 3, 2 * NFF, 128).transpose(0, 3, 1, 2)
    shared["conv_wT"] = np.ascontiguousarray(cwt)
    shared["conv_bT"] = fm(inp["conv_b"], 2 * NFF)
    rpb = np.asarray(inp["rpb"], f32)[:L]
    shared["rpbG"] = np.ascontiguousarray(rpb[:, :, C["_dr"], C["_dc"]])
    for k in ("gmask", "ropeC", "ropeS", "Rm", "CSm", "dft256", "dftN", "kaugc", "qaugc"):
        shared[k] = C[k]
    x = np.asarray(inp["x"], f32)
    ctx = np.asarray(inp["ctx"], f32)
    c = np.asarray(inp["c"], f32)
    cc = np.asarray(inp["c_ctx"], f32)
    maps = []
    for core in range(ncores):
        m = dict(shared)
        hs = []
        for j in range(NB):
            bi = core * NB + j
            hcat = np.concatenate([x[bi], ctx[bi]], 0)
            hs.append(hcat.T.reshape(8, 128, T))
        m["h0T"] = np.ascontiguousarray(np.stack(hs, 0))
        cs = np.stack([c[core * NB + j] for j in range(NB)] + [cc], 0)
        m["cT"] = np.ascontiguousarray(cs.reshape(3, 8, 128).transpose(2, 1, 0))
        maps.append(m)
    return maps


_PROG = {}


def kernel(**inputs):
    L, NB, ncores = DEPTH, 2, 8
    if "nc" not in _PROG:
        _PROG["nc"] = build_program(L, NB)
    nc = _PROG["nc"]
    maps = _prep_inputs(inputs, L, NB, ncores)
    res = run_bass_kernel_spmd(nc, maps, core_ids=list(range(ncores)))
    outs = []
    for core in range(ncores):
        o = res.results[core]["outT"]
        for j in range(NB):
            outs.append(np.asarray(o[j], np.float32).reshape(D, NLAT).T)
    return np.ascontiguousarray(np.stack(outs, 0))
```
